# Optimizing a Trainium2 kernel written in Bass

```python
import jax, jax.numpy as jnp
from jax import lax
import numpy as np

D_MODEL = 2048
BATCH = 2
SEQ = 4096
DEPTH = 1

CHUNK = 64

GLA_HEADS = 4
GLA_DK = 128
GLA_DV = 256
GLA_QK = GLA_HEADS * GLA_DK
GLA_V = GLA_HEADS * GLA_DV
GLA_LORA = 16
GLA_TAU = 16.0

RWKV_HEADS = 16
RWKV_HD = 64
RWKV_W = RWKV_HEADS * RWKV_HD
DECAY_LORA = 96
AAA_LORA = 96
GATE_LORA = 256
GN_EPS = 64e-5

N_BRANCH = 2
D_FF = 5632
NORM_EPS = 1e-6

GLA_SPLITS = (GLA_QK, GLA_QK, GLA_V, GLA_V, GLA_LORA)
GLA_IN = 2 * GLA_QK + 2 * GLA_V + GLA_LORA
RWKV_SPLITS = (RWKV_W, RWKV_W, RWKV_W, DECAY_LORA, AAA_LORA, GATE_LORA)
RWKV_IN = 3 * RWKV_W + DECAY_LORA + AAA_LORA + GATE_LORA
D_IN = GLA_IN + RWKV_IN + N_BRANCH * D_MODEL
BRANCH_IN = GLA_V + RWKV_W

kernel_name = "hybrid_gla_rwkv7_macaron_block"


def _split(p, sizes):
    out, off = [], 0
    for s in sizes:
        out.append(p[..., off:off + s])
        off += s
    return out


def rmsnorm(x, g):
    xf = x.astype(jnp.float32)
    y = xf * lax.rsqrt(jnp.mean(xf * xf, axis=-1, keepdims=True) + NORM_EPS)
    return (y * g.astype(jnp.float32)).astype(x.dtype)


def swiglu(h, wg, wu, wd):
    return (jax.nn.silu(h @ wg) * (h @ wu)) @ wd


def token_shift(p, mu):
    prev = jnp.pad(p, ((0, 0), (1, 0), (0, 0)))[:, :-1]
    return p + mu * (prev - p)


def gla_branch(q, k, v, r, a_down, w_a2, b_a, gn_w):
    B, S, _ = q.shape
    nc = S // CHUNK
    f32 = jnp.float32
    log_alpha = jax.nn.log_sigmoid(a_down.astype(f32) @ w_a2.astype(f32) + b_a.astype(f32)) / GLA_TAU
    shp = (B, nc, CHUNK, GLA_HEADS, GLA_DK)
    qf = q.astype(f32).reshape(shp) * (GLA_DK ** -0.5)
    kf = k.astype(f32).reshape(shp)
    vf = v.astype(f32).reshape(B, nc, CHUNK, GLA_HEADS, GLA_DV)
    cum = jnp.cumsum(log_alpha.reshape(shp), axis=2)
    total = cum[:, :, -1]
    kdec = kf * jnp.exp(total[:, :, None] - cum)
    u = jnp.einsum('bnchk,bnchv->nbhkv', kdec, vf)

    def step(state, inp):
        lt, uc = inp
        state = jnp.exp(lt)[..., None] * state + uc
        return state, state

    s0 = jnp.zeros((B, GLA_HEADS, GLA_DK, GLA_DV), f32)
    _, states = lax.scan(step, s0, (jnp.moveaxis(total, 1, 0), u))
    o = jnp.einsum('bnchk,nbhkv->bnchv', qf, states)
    o = o * lax.rsqrt(jnp.mean(o * o, axis=-1, keepdims=True) + NORM_EPS) * gn_w.astype(f32)
    o = o.reshape(B, S, GLA_V) * jax.nn.silu(r.astype(f32))
    return o.astype(q.dtype)


def rwkv7_branch(r, k, v, wd, ad, gd, w0, w_w2, a0, w_a2, w_g2, k_k, k_a, r_k, lnx_w, lnx_b):
    B, S, _ = r.shape
    f32 = jnp.float32
    r, k, v = r.astype(f32), k.astype(f32), v.astype(f32)
    w_raw = w0.astype(f32) + jnp.tanh(wd.astype(f32)) @ w_w2.astype(f32)
    log_w = -jnp.exp(-jax.nn.softplus(-w_raw) - 0.5)
    a = jax.nn.sigmoid(a0.astype(f32) + ad.astype(f32) @ w_a2.astype(f32))
    g = jax.nn.sigmoid(gd.astype(f32)) @ w_g2.astype(f32)
    hs = (B, S, RWKV_HEADS, RWKV_HD)
    kk = (k * k_k.astype(f32)).reshape(hs)
    kk = kk / jnp.maximum(jnp.linalg.norm(kk, axis=-1, keepdims=True), 1e-12)
    k = k * (1.0 + (a - 1.0) * k_a.astype(f32))
    rh, kh, vh, ah = r.reshape(hs), k.reshape(hs), v.reshape(hs), a.reshape(hs)
    decay = jnp.exp(log_w).reshape(hs)
    b = kk * ah

    def step(state, inp):
        r_t, w_t, k_t, v_t, kk_t, b_t = inp
        sa = jnp.einsum('bhvk,bhk->bhv', state, kk_t)
        state = (state * w_t[:, :, None, :] - sa[..., None] * b_t[:, :, None, :]
                 + v_t[..., None] * k_t[:, :, None, :])
        return state, jnp.einsum('bhvk,bhk->bhv', state, r_t)

    xs = tuple(jnp.moveaxis(t, 1, 0) for t in (rh, decay, kh, vh, kk, b))
    s0 = jnp.zeros((B, RWKV_HEADS, RWKV_HD, RWKV_HD), f32)
    _, y = lax.scan(step, s0, xs)
    y = jnp.moveaxis(y, 0, 1)
    mu = jnp.mean(y, axis=-1, keepdims=True)
    var = jnp.mean(jnp.square(y - mu), axis=-1, keepdims=True)
    yn = ((y - mu) * lax.rsqrt(var + GN_EPS)).reshape(B, S, RWKV_W)
    yn = yn * lnx_w.astype(f32) + lnx_b.astype(f32)
    bonus = (jnp.sum(rh * kh * r_k.astype(f32), axis=-1, keepdims=True) * vh).reshape(B, S, RWKV_W)
    return ((yn + bonus) * g).astype(r.dtype)


def hybrid_mixer(h, w_in, gla_w_a2, gla_b_a, gla_gn_w, rwkv_mu, rwkv_w0, rwkv_w_w2,
                 rwkv_a0, rwkv_w_a2, rwkv_w_g2, rwkv_k_k, rwkv_k_a, rwkv_r_k,
                 rwkv_lnx_w, rwkv_lnx_b, gate_b, w_branch, w_out):
    p = h @ w_in
    gla_p = p[..., :GLA_IN]
    rw_p = token_shift(p[..., GLA_IN:GLA_IN + RWKV_IN], rwkv_mu)
    gate_p = p[..., GLA_IN + RWKV_IN:]
    gq, gk, gv, gr, gad = _split(gla_p, GLA_SPLITS)
    rr, rk, rv, rwd, rad, rgd = _split(rw_p, RWKV_SPLITS)
    o_gla = gla_branch(gq, gk, gv, gr, gad, gla_w_a2, gla_b_a, gla_gn_w)
    o_rw = rwkv7_branch(rr, rk, rv, rwd, rad, rgd, rwkv_w0, rwkv_w_w2, rwkv_a0, rwkv_w_a2,
                        rwkv_w_g2, rwkv_k_k, rwkv_k_a, rwkv_r_k, rwkv_lnx_w, rwkv_lnx_b)
    gates = jax.nn.sigmoid((gate_p + gate_b).astype(jnp.float32))
    y_gla = (o_gla @ w_branch[:GLA_V]).astype(jnp.float32)
    y_rw = (o_rw @ w_branch[GLA_V:]).astype(jnp.float32)
    merged = gates[..., :D_MODEL] * y_gla + gates[..., D_MODEL:] * y_rw
    return merged.astype(h.dtype) @ w_out


def setup_inputs(seed: int = 0) -> dict:
    key = jax.random.key(seed)
    ks = iter(jax.random.split(key, 40))
    L, D = DEPTH, D_MODEL

    def nrm(shape, scale):
        return scale * jax.random.normal(next(ks), shape, jnp.float32)

    def gain(shape):
        return 1.0 + nrm(shape, 0.02)

    return {
        "x": nrm((BATCH, SEQ, D), 1.0),
        "ffn1_norm": gain((L, D)),
        "ffn1_wg": nrm((L, D, D_FF), D ** -0.5),
        "ffn1_wu": nrm((L, D, D_FF), D ** -0.5),
        "ffn1_wd": nrm((L, D_FF, D), D_FF ** -0.5),
        "mix_norm": gain((L, D)),
        "w_in": nrm((L, D, D_IN), D ** -0.5),
        "gla_w_a2": nrm((L, GLA_LORA, GLA_QK), GLA_LORA ** -0.5),
        "gla_b_a": nrm((L, GLA_QK), 0.1),
        "gla_gn_w": gain((L, GLA_DV)),
        "rwkv_mu": jax.random.uniform(next(ks), (L, RWKV_IN), jnp.float32, 0.0, 1.0),
        "rwkv_w0": -2.0 + nrm((L, RWKV_W), 0.5),
        "rwkv_w_w2": nrm((L, DECAY_LORA, RWKV_W), 0.3 * DECAY_LORA ** -0.5),
        "rwkv_a0": nrm((L, RWKV_W), 0.1),
        "rwkv_w_a2": nrm((L, AAA_LORA, RWKV_W), 0.3 * AAA_LORA ** -0.5),
        "rwkv_w_g2": nrm((L, GATE_LORA, RWKV_W), GATE_LORA ** -0.5),
        "rwkv_k_k": 0.85 + nrm((L, RWKV_W), 0.05),
        "rwkv_k_a": gain((L, RWKV_W)),
        "rwkv_r_k": nrm((L, RWKV_HEADS, RWKV_HD), 0.1),
        "rwkv_lnx_w": gain((L, RWKV_W)),
        "rwkv_lnx_b": nrm((L, RWKV_W), 0.01),
        "gate_b": nrm((L, N_BRANCH * D), 0.1),
        "w_branch": nrm((L, BRANCH_IN, D), GLA_V ** -0.5),
        "w_out": nrm((L, D, D), D ** -0.5),
        "ffn2_norm": gain((L, D)),
        "ffn2_wg": nrm((L, D, D_FF), D ** -0.5),
        "ffn2_wu": nrm((L, D, D_FF), D ** -0.5),
        "ffn2_wd": nrm((L, D_FF, D), D_FF ** -0.5),
        "final_norm": gain((D,)),
    }


def reference(x, ffn1_norm, ffn1_wg, ffn1_wu, ffn1_wd, mix_norm, w_in, gla_w_a2, gla_b_a,
              gla_gn_w, rwkv_mu, rwkv_w0, rwkv_w_w2, rwkv_a0, rwkv_w_a2, rwkv_w_g2, rwkv_k_k,
              rwkv_k_a, rwkv_r_k, rwkv_lnx_w, rwkv_lnx_b, gate_b, w_branch, w_out,
              ffn2_norm, ffn2_wg, ffn2_wu, ffn2_wd, final_norm):
    for l in range(DEPTH):
        h = rmsnorm(x, ffn1_norm[l])
        x = x + 0.5 * swiglu(h, ffn1_wg[l], ffn1_wu[l], ffn1_wd[l])
        h = rmsnorm(x, mix_norm[l])
        x = x + hybrid_mixer(h, w_in[l], gla_w_a2[l], gla_b_a[l], gla_gn_w[l], rwkv_mu[l],
                             rwkv_w0[l], rwkv_w_w2[l], rwkv_a0[l], rwkv_w_a2[l], rwkv_w_g2[l],
                             rwkv_k_k[l], rwkv_k_a[l], rwkv_r_k[l], rwkv_lnx_w[l], rwkv_lnx_b[l],
                             gate_b[l], w_branch[l], w_out[l])
        h = rmsnorm(x, ffn2_norm[l])
        x = x + 0.5 * swiglu(h, ffn2_wg[l], ffn2_wu[l], ffn2_wd[l])
    return rmsnorm(x, final_norm)
```

```python
import numpy as np
import concourse.bass as bass
import concourse.mybir as mybir
from concourse.bass_utils import run_bass_kernel_spmd

F32 = mybir.dt.float32
F32R = mybir.dt.float32r
BF16 = mybir.dt.bfloat16
U8 = mybir.dt.uint8
I32 = mybir.dt.int32
AF = mybir.ActivationFunctionType
ALU = mybir.AluOpType
AX = mybir.AxisListType

NORM_EPS = 1e-6
GN_EPS = 64e-5


class Cfg:
    def __init__(self, D=2048, FF=5632, T=1024, NQ=4, TH=512, stages="all"):
        self.D, self.FF, self.T, self.NQ = D, FF, T, NQ
        self.KT = D // 128
        self.FT = FF // 128
        self.FQ = self.FT // NQ
        self.TH = min(TH, T)
        self.NH = T // self.TH
        self.SEQ = 4 * T
        self.DG = min(2, self.KT)
        self.stages = stages


class Buf:
    __slots__ = ("name", "w", "r", "excl")

    def __init__(self, name, excl=False):
        self.name, self.w, self.r, self.excl = name, None, {}, excl


class Q:
    def __init__(self, name, sem, kind):
        self.name, self.sem, self.kind = name, sem, kind
        self.ops = []
        self.cnt = 0
        self.known = {}


class DmaRing:
    def __init__(self, name, sems):
        self.name, self.sems = name, sems
        self.n = 0
        self.last = [None] * len(sems)


class _Rec:
    def __getattr__(self, name):
        return lambda *a, **k: (name, a, k)


_REC = _Rec()


class Sched:
    def __init__(self, nc):
        self.nc = nc
        self.q = {}
        for nm, kind in (("pe", "pe"), ("act", "act"), ("dve", "dve"), ("pool", "pool"), ("sp", "sp")):
            self.q[nm] = Q(nm, nc.alloc_semaphore("sem_" + nm), kind)
        self.rings = {
            "sp": DmaRing("sp", [nc.alloc_semaphore("dsp%d" % i) for i in range(8)]),
            "pool": DmaRing("pool", [nc.alloc_semaphore("dpl%d" % i) for i in range(12)]),
            "act": DmaRing("act", [nc.alloc_semaphore("dac%d" % i) for i in range(6)]),
        }

    @staticmethod
    def _key(sem):
        return sem.num

    def _need(self, q, waits, tok):
        if tok is None:
            return
        sem, val = tok
        k = sem.num
        if q.known.get(k, 0) >= val:
            return
        if k not in waits or waits[k][1] < val:
            waits[k] = (sem, val)

    def op(self, qn, fn, reads=(), writes=(), dma=False):
        q = self.q[qn]
        waits = {}
        excl = [b for b in reads if b.excl]
        if excl:
            reads = [b for b in reads if not b.excl]
            writes = list(writes) + excl
        for b in reads:
            self._need(q, waits, b.w)
        for b in writes:
            self._need(q, waits, b.w)
            for t in b.r.values():
                self._need(q, waits, t)
        if dma:
            ring = self.rings[qn]
            slot = ring.n % len(ring.sems)
            self._need(q, waits, ring.last[slot])
            tok = (ring.sems[slot], 16 * (ring.n // len(ring.sems) + 1))
            ring.last[slot] = tok
            ring.n += 1
            inc = (tok[0], 16)
        else:
            if q.kind == "pe":
                waits.pop(q.sem.num, None)
            q.cnt += 1
            tok = (q.sem, q.cnt)
            inc = (q.sem, 1)
        wl = list(waits.values())
        for sem, val in wl:
            q.known[sem.num] = max(q.known.get(sem.num, 0), val)
        q.ops.append((wl, fn(_REC), inc))
        for b in reads:
            k = tok[0].num
            if k not in b.r or b.r[k][1] < tok[1]:
                b.r[k] = tok
        for b in writes:
            b.w = tok
            b.r = {}
        return tok

    def wait_all(self, qn, toks):
        q = self.q[qn]
        waits = {}
        for t in toks:
            self._need(q, waits, t)
        wl = list(waits.values())
        for sem, val in wl:
            q.known[sem.num] = max(q.known.get(sem.num, 0), val)
        if wl:
            q.ops.append((wl, None, None))

    def emit(self):
        nc = self.nc
        with nc.Block() as block:
            def run(q):
                def body(e):
                    for wl, fn, inc in q.ops:
                        for sem, val in wl:
                            e.wait_ge(sem, val)
                        if fn is not None:
                            ins = getattr(e, fn[0])(*fn[1], **fn[2])
                            ins.then_inc(inc[0], inc[1])
                return body
            block.tensor(run(self.q["pe"]))
            block.scalar(run(self.q["act"]))
            block.vector(run(self.q["dve"]))
            block.gpsimd(run(self.q["pool"]))
            block.sync(run(self.q["sp"]))


class Arena:
    def __init__(self, nc, lo, hi):
        self.nc, self.lo, self.hi = nc, lo, hi
        self.cur = lo
        self.n = 0
        self.marks = []

    def alloc(self, name, shape, dtype):
        esz = {F32: 4, F32R: 4, BF16: 2, U8: 1, I32: 4}[dtype]
        nbytes = esz
        for s in shape[1:]:
            nbytes *= s
        off = (self.cur + 31) // 32 * 32
        assert off + nbytes <= self.hi, ("SBUF arena overflow", name, off, nbytes, self.hi)
        self.cur = off + nbytes
        self.n += 1
        return self.nc.alloc_sbuf_tensor_at("%s_%d" % (name, self.n), list(shape), dtype, offset=off)

    def push(self):
        self.marks.append(self.cur)

    def pop(self):
        self.cur = self.marks.pop()


class _Stop(Exception):
    pass


class Prog:
    def stop(self, label):
        if getattr(self.cfg, "stop_at", None) == label:
            raise _Stop()

    def __init__(self, cfg):
        self.cfg = cfg
        nc = bass.Bass("TRN2", target_bir_lowering=False)
        self.nc = nc
        self.S = Sched(nc)
        lo = (nc.sbuf_base + 31) // 32 * 32
        hi = nc.sbuf_top // 32 * 32
        self.fence = nc.alloc_sbuf_tensor("arena_fence", [128, hi - lo], U8)
        self.A = Arena(nc, lo, hi)
        self.psum = [nc.alloc_psum_tensor("ps%d" % i, [128, 512], F32) for i in range(8)]
        self.psb = [Buf("ps%d" % i, excl=True) for i in range(8)]
        self.dram = {}

    def din(self, name, shape, dtype=F32):
        t = self.nc.dram_tensor(name, list(shape), dtype, kind="ExternalInput")
        self.dram[name] = t
        return t.ap()

    def dout(self, name, shape, dtype=F32):
        t = self.nc.dram_tensor(name, list(shape), dtype, kind="ExternalOutput")
        self.dram[name] = t
        return t.ap()

    def barrier(self):
        S = self.S
        toks = []
        for q in S.q.values():
            if q.cnt:
                toks.append((q.sem, q.cnt))
        for r in S.rings.values():
            for t in r.last:
                if t is not None:
                    toks.append(t)
        for qn in S.q:
            S.wait_all(qn, toks)

    def rmsnorm_fm(self, xT, xB, gT, outT, outB, out_dtype_note=None):
        cfg, S, A = self.cfg, self.S, self.A
        KT, T, TH, NH = cfg.KT, cfg.T, cfg.TH, cfg.NH
        A.push()
        sq = [A.alloc("sq", [128, T], F32R) for _ in range(2)]
        sqB = [Buf("sq0"), Buf("sq1")]
        rstd = A.alloc("rstd", [128, T], F32)
        rstdB = Buf("rstd")
        ones = self.onesR
        for kt in range(KT):
            s = kt % 2
            S.op("act", lambda e, kt=kt, s=s: e.activation(out=sq[s][:], in_=xT[:, kt, :], func=AF.Square),
                 reads=[xB[kt][h] for h in range(NH)], writes=[sqB[s]])
            for h in range(NH):
                S.op("pe", lambda e, kt=kt, s=s, h=h: e.matmul(self.psum[h][:, :TH], lhsT=ones[:], rhs=sq[s][:, h * TH:(h + 1) * TH],
                                                                start=(kt == 0), stop=(kt == KT - 1)),
                     reads=[sqB[s], self.onesB], writes=[self.psb[h]])
        for h in range(NH):
            S.op("act", lambda e, h=h: e.activation(out=rstd[:, h * TH:(h + 1) * TH], in_=self.psum[h][:, :TH], func=AF.Sqrt,
                                                    scale=1.0 / cfg.D, bias=self.eps_ap[:]),
                 reads=[self.psb[h], self.epsB], writes=[rstdB])
        S.op("dve", lambda e: e.reciprocal(out=rstd[:], in_=rstd[:]), reads=[rstdB], writes=[rstdB])
        for kt in range(KT):
            S.op("dve", lambda e, kt=kt: e.scalar_tensor_tensor(out=outT[:, kt, :], in0=xT[:, kt, :], scalar=gT[:, kt:kt + 1],
                                                                in1=rstd[:], op0=ALU.mult, op1=ALU.mult),
                 reads=[xB[kt][h] for h in range(NH)] + [rstdB, self.vecB], writes=[outB[kt]])
        A.pop()
        self.barrier()

    def ffn(self, xT, xB, gT, wgu, wd, hT=None, hB=None):
        cfg, S, A = self.cfg, self.S, self.A
        KT, T, TH, NH, FQ, NQ = cfg.KT, cfg.T, cfg.TH, cfg.NH, cfg.FQ, cfg.NQ
        A.push()
        if hT is None:
            hT = A.alloc("hT", [128, KT, T], BF16)
            hB = [Buf("h%d" % k) for k in range(KT)]
        self.rmsnorm_fm(xT, xB, gT, hT, hB)
        aT = A.alloc("aT", [128, FQ, T], BF16)
        aB = [[Buf("a") for _ in range(NH)] for _ in range(FQ)]
        NW = 2
        wsl = [A.alloc("wgu", [128, 2 * KT * 128], BF16) for _ in range(NW)]
        wslB = [Buf("wgu%d" % i) for i in range(NW)]
        ND = 2
        DG = cfg.DG
        dsl = [A.alloc("wd", [128, DG, FQ * 128], BF16) for _ in range(ND)]
        dslB = [Buf("wd%d" % i) for i in range(ND)]
        sg = [A.alloc("sg", [128, TH], BF16) for _ in range(2)]
        sgB = [Buf("sg0"), Buf("sg1")]
        wi = 0
        di = 0
        ei = 0
        for q in range(NQ):
            for f in range(FQ):
                fc = q * FQ + f
                s = wi % NW
                wi += 1
                S.op("pool", lambda e, s=s, fc=fc: e.dma_start(out=wsl[s][:], in_=wgu[fc]), writes=[wslB[s]], dma=True)
                pb = 4 * (fc % 2)
                for m in range(2):
                    for h in range(NH):
                        bank = pb + 2 * m + h
                        for kt in range(KT):
                            S.op("pe", lambda e, s=s, m=m, h=h, kt=kt, bank=bank: e.matmul(
                                self.psum[bank][:, :TH], lhsT=wsl[s][:, (m * KT + kt) * 128:(m * KT + kt + 1) * 128],
                                rhs=hT[:, kt, h * TH:(h + 1) * TH], start=(kt == 0), stop=(kt == KT - 1)),
                                reads=[wslB[s], hB[kt]], writes=[self.psb[bank]])
                for h in range(NH):
                    es = ei % 2
                    ei += 1
                    S.op("act", lambda e, es=es, h=h, pb=pb: e.activation(out=sg[es][:], in_=self.psum[pb + h][:, :TH], func=AF.Silu),
                         reads=[self.psb[pb + h]], writes=[sgB[es]])
                    S.op("dve", lambda e, es=es, h=h, pb=pb, f=f: e.tensor_tensor(out=aT[:, f, h * TH:(h + 1) * TH], in0=sg[es][:],
                                                                               in1=self.psum[pb + 2 + h][:, :TH], op=ALU.mult),
                         reads=[sgB[es], self.psb[pb + 2 + h]], writes=[aB[f][h]])
            for dg in range(KT // DG):
                s = di % ND
                di += 1
                S.op("pool", lambda e, s=s, q=q, dg=dg: e.dma_start(out=dsl[s][:], in_=wd[q, dg]), writes=[dslB[s]], dma=True)
                for d in range(DG):
                    dt = dg * DG + d
                    for h in range(NH):
                        bank = (dt * NH + h) % 8
                        for f in range(FQ):
                            S.op("pe", lambda e, s=s, d=d, f=f, h=h, bank=bank: e.matmul(
                                self.psum[bank][:, :TH], lhsT=dsl[s][:, d, f * 128:(f + 1) * 128],
                                rhs=aT[:, f, h * TH:(h + 1) * TH], start=(f == 0), stop=(f == FQ - 1)),
                                reads=[dslB[s], aB[f][h]], writes=[self.psb[bank]])
                        S.op("dve", lambda e, dt=dt, h=h, bank=bank: e.scalar_tensor_tensor(
                            out=xT[:, dt, h * TH:(h + 1) * TH], in0=self.psum[bank][:, :TH], scalar=0.5,
                            in1=xT[:, dt, h * TH:(h + 1) * TH], op0=ALU.mult, op1=ALU.add),
                            reads=[self.psb[bank], xB[dt][h]], writes=[xB[dt][h]])
        A.pop()
        self.barrier()

    def tl(self, name, shape, dtype):
        return self.A.alloc(name, shape, dtype), Buf(name)

    def ct(self, key, shape, dtype):
        if key not in self._cache:
            self._cache[key] = self.tl(key, shape, dtype)
        return self._cache[key]

    def next_bank(self):
        b = self._bank % 4
        self._bank += 1
        return b

    def small(self, n=1):
        i = self._small
        self._small += 1
        bank = 4 + i % 4
        c0 = ((i // 4) % 2) * 256 if n == 2 else ((i // 4) % 4) * 128
        return self.psum[bank], c0, [self.psb[bank]]

    def mixer_setup(self):
        cfg, S, A = self.cfg, self.S, self.A
        T, MB = cfg.T, cfg.TH
        self._bank = 0
        self._small = 0
        self.pssb = [Buf("pss%d" % i) for i in range(16)]
        NMV = MV_N
        mv_d = self.din("mvecs", [128, NMV])
        NCON = 960 + MB
        con_d = self.din("consts", [128, NCON])
        gwa2_d = self.din("gla_wa2", [16, 512])
        ww2_d = self.din("rw_ww2", [96, 1024])
        wa2_d = self.din("rw_wa2", [96, 1024])
        wg2_d = self.din("rw_wg2", [128, 2, 1024])
        self.mv, self.mvB = self.tl("mv", [128, NMV], F32)
        self.gneps, self.gnepsB = self.tl("gneps", [128, 1], F32)
        A.push()
        self.con, self.conB = self.tl("con", [128, NCON], F32)
        self.conR, self.conRB = self.tl("conR", [128, 256], F32)
        self.mvd, self.mvdB = self.tl("mvd", [128, 12], F32)
        self.gwa2, self.gwa2B = self.tl("gwa2", [16, 512], F32)
        self.ww2, self.ww2B = self.tl("ww2", [96, 1024], F32)
        self.wa2, self.wa2B = self.tl("wa2", [96, 1024], F32)
        self.wg2, self.wg2B = self.tl("wg2", [128, 2, 1024], F32)
        self.Sg = [self.tl("Sg%d" % h, [128, 2, 256], F32) for h in range(4)]
        self.Tst = [self.tl("Tst%d" % h, [64, 2, 128], F32) for h in range(16)]
        self.carry, self.carryB = self.tl("carry", [128, 28], F32)
        stg, stgB = self.tl("stg", [128, 2, 1024], F32)
        S.op("sp", lambda e: e.dma_start(out=self.mv[:], in_=mv_d), writes=[self.mvB], dma=True)
        S.op("sp", lambda e: e.dma_start(out=self.con[:], in_=con_d), writes=[self.conB], dma=True)
        S.op("dve", lambda e: e.tensor_copy(out=self.conR[:].bitcast(F32R), in_=self.con[:, 0:256]), reads=[self.conB], writes=[self.conRB])
        S.op("dve", lambda e: e.tensor_scalar(out=self.mvd[:, 0:4], in0=self.mv[:, MV["gla_ba"]:MV["gla_ba"] + 4], scalar1=-1.0, scalar2=None,
                                              op0=ALU.mult), reads=[self.mvB], writes=[self.mvdB])
        S.op("dve", lambda e: e.tensor_scalar(out=self.mvd[:, 4:12], in0=self.mv[:, MV["k_a"]:MV["k_a"] + 8], scalar1=-1.0, scalar2=1.0,
                                              op0=ALU.mult, op1=ALU.add), reads=[self.mvB], writes=[self.mvdB])
        S.op("dve", lambda e: e.memset(self.gneps[:], GN_EPS), writes=[self.gnepsB])
        for (dst, dB, src, np_, shp) in ((self.gwa2, self.gwa2B, gwa2_d, 16, None), (self.ww2, self.ww2B, ww2_d, 96, None),
                                         (self.wa2, self.wa2B, wa2_d, 96, None)):
            n = 512 if np_ == 16 else 1024
            S.op("sp", lambda e, src=src, np_=np_, n=n: e.dma_start(out=stg[:np_, 0, :n], in_=src), writes=[stgB], dma=True)
            S.op("dve", lambda e, dst=dst, np_=np_, n=n: e.tensor_copy(out=dst[:].bitcast(F32R), in_=stg[:np_, 0, :n]), reads=[stgB], writes=[dB])
        S.op("sp", lambda e: e.dma_start(out=stg[:], in_=wg2_d), writes=[stgB], dma=True)
        S.op("dve", lambda e: e.tensor_copy(out=self.wg2[:].bitcast(F32R), in_=stg[:]), reads=[stgB], writes=[self.wg2B])
        S.op("pool", lambda e: e.memset(stg[:, 0, 0:512], 0.0), writes=[stgB])
        for h in range(4):
            S.op("dve", lambda e, h=h: e.tensor_copy(out=self.Sg[h][0][:].rearrange("p a b -> p (a b)").bitcast(F32R), in_=stg[:, 0, 0:512]),
                 reads=[stgB], writes=[self.Sg[h][1]])
        for h in range(16):
            S.op("dve", lambda e, h=h: e.tensor_copy(out=self.Tst[h][0][:].rearrange("p a b -> p (a b)").bitcast(F32R), in_=stg[0:64, 0, 0:256]),
                 reads=[stgB], writes=[self.Tst[h][1]])
        S.op("pool", lambda e: e.memset(self.carry[:], 0.0), writes=[self.carryB])
        self.gcur = [0] * 4
        self.tcur = [0] * 16

    def c_ident(self):
        return self.con[:, 0:128]

    def c_identR(self):
        return self.conR[:, 0:128].bitcast(F32R)

    def c_bdonesR(self):
        return self.conR[:, 128:256].bitcast(F32R)

    def proj_fm(self, w, wB, c0, m, hT, hB, t0, evac):
        cfg, S = self.cfg, self.S
        KT, MB = cfg.KT, cfg.TH
        bank = self.next_bank()
        for kt in range(KT):
            S.op("pe", lambda e, kt=kt, bank=bank: e.matmul(self.psum[bank][:m, :MB], lhsT=w[:, kt, c0:c0 + m], rhs=hT[:, kt, t0:t0 + MB],
                                                            start=(kt == 0), stop=(kt == KT - 1)),
                 reads=[wB, hB[kt]], writes=[self.psb[bank]])
        evac(self.psum[bank][:m, :MB], self.psb[bank])

    def proj_shift(self, w, wB, c0, m, hT, hB, t0, cidx, mucol, dst, dstB):
        cfg, S = self.cfg, self.S
        MB = cfg.TH
        P, PB = self.Pt[self._pi % 2]
        Dt, DB = self.Dt[self._pi % 2]
        self._pi += 1
        S.op("pool", lambda e: e.tensor_copy(out=P[:m, 0:1], in_=self.carry[:m, cidx:cidx + 1]), reads=[self.carryB], writes=[PB])
        self.proj_fm(w, wB, c0, m, hT, hB, t0,
                     lambda ps, psB: S.op("act", lambda e: e.activation(out=P[:m, 1:MB + 1], in_=ps, func=AF.Copy), reads=[psB], writes=[PB]))
        S.op("pool", lambda e: e.tensor_copy(out=self.carry[:m, cidx:cidx + 1], in_=P[:m, MB:MB + 1]), reads=[PB], writes=[self.carryB])
        S.op("dve", lambda e: e.tensor_tensor(out=Dt[:m, :], in0=P[:m, 0:MB], in1=P[:m, 1:MB + 1], op=ALU.subtract), reads=[PB], writes=[DB])
        S.op("dve", lambda e: e.scalar_tensor_tensor(out=dst, in0=Dt[:m, :], scalar=self.mv[:m, mucol:mucol + 1], in1=P[:m, 1:MB + 1],
                                                     op0=ALU.mult, op1=ALU.add), reads=[DB, PB, self.mvB], writes=[dstB])

    def mixer_block(self, s, hb, hT, hB, o_all):
        cfg, S, A = self.cfg, self.S, self.A
        KT, MB = cfg.KT, cfg.TH
        t0 = hb * MB
        NCK = MB // 64
        NTT = MB // 128
        A.push()
        self._pi = 0
        self.Pt = [self.tl("Pt", [128, MB + 1], F32) for _ in range(2)]
        self.Dt = [self.tl("Dt", [128, MB], F32) for _ in range(2)]
        mv = self.mv
        cmask = self.con[:, 960:960 + MB]
        adT, adB = self.tl("adT", [16, MB], F32)
        twd, twdB = self.tl("twd", [96, MB], F32)
        adS, adSB = self.tl("adS", [96, MB], F32)
        sgd, sgdB = self.tl("sgd", [128, 2, MB], F32)
        A.push()
        wl, wlB = self.tl("wl", [128, KT, 464], BF16)
        S.op("pool", lambda e: e.dma_start(out=wl[:].rearrange("p k c -> p (k c)"), in_=self.w_lora_d), writes=[wlB], dma=True)
        tmp, tmpB = self.tl("ltmp", [128, MB], F32)
        self.proj_fm(wl, wlB, 0, 16, hT, hB, t0,
                     lambda ps, psB: S.op("act", lambda e: e.activation(out=adT[:].bitcast(F32R), in_=ps, func=AF.Copy), reads=[psB], writes=[adB]))
        self.proj_shift(wl, wlB, 16, 96, hT, hB, t0, 0, MV["mu_wd"], tmp[:96, :], tmpB)
        S.op("act", lambda e: e.activation(out=twd[:].bitcast(F32R), in_=tmp[:96, :], func=AF.Tanh), reads=[tmpB], writes=[twdB])
        self.proj_shift(wl, wlB, 112, 96, hT, hB, t0, 1, MV["mu_ad"], tmp[:96, :], tmpB)
        S.op("act", lambda e: e.activation(out=adS[:].bitcast(F32R), in_=tmp[:96, :], func=AF.Copy), reads=[tmpB], writes=[adSB])
        for g2 in range(2):
            self.proj_shift(wl, wlB, 208 + 128 * g2, 128, hT, hB, t0, 2 + g2, MV["mu_gd"] + g2, tmp[:, :], tmpB)
            S.op("act", lambda e, g2=g2: e.activation(out=sgd[:, g2, :].bitcast(F32R), in_=tmp[:, :], func=AF.Sigmoid), reads=[tmpB], writes=[sgdB])
        A.pop()
        self.barrier()
        self.stop("lora")
        A.push()
        self._cache = {}
        for h in range(4):
            self.gla_head(s, h, hT, hB, t0, adT, adB, cmask, o_all)
            self.stop("gla")
        A.pop()
        self.barrier()
        A.push()
        self._cache = {}
        for pp in range(8):
            self.rw_pair(s, pp, hT, hB, t0, twd, twdB, adS, adSB, sgd, sgdB, cmask, o_all)
            self.stop("rw")
        A.pop()
        self.barrier()
        A.pop()
        self.barrier()

    def gla_head(self, s, h, hT, hB, t0, adT, adB, cmask, o_all):
        cfg, S, A = self.cfg, self.S, self.A
        KT, MB = cfg.KT, cfg.TH
        NCK, NTT = MB // 64, MB // 128
        mv = self.mv
        w, wB = self.ct("wg", [128, KT, 768], BF16)
        S.op("pool", lambda e: e.dma_start(out=w[:].rearrange("p k c -> p (k c)"), in_=self.w_gla_d[h]), writes=[wB], dma=True)
        qT, qB = self.ct("qT", [128, MB], F32)
        kT, kB = self.ct("kT", [128, MB], F32)
        rS, rSB = self.ct("rS", [128, 2, MB], F32)
        vtm, vtmB = self.ct("vtm", [128, NTT, 256], F32)
        L, LB = self.ct("L", [128, MB], F32)
        CL, CLB = self.ct("CL", [128, MB], F32)
        KD, KDB = self.ct("KD", [128, MB], F32)
        kdtm, kdtmB = self.ct("kdtm", [128, NTT, 128], F32)
        ET, ETB = self.ct("ET", [128, NCK], F32)
        self.proj_fm(w, wB, 0, 128, hT, hB, t0,
                     lambda ps, psB: S.op("act", lambda e: e.activation(out=qT[:].bitcast(F32R), in_=ps, func=AF.Copy, scale=128.0 ** -0.5),
                                          reads=[psB], writes=[qB]))
        self.proj_fm(w, wB, 128, 128, hT, hB, t0,
                     lambda ps, psB: S.op("act", lambda e: e.activation(out=kT[:], in_=ps, func=AF.Copy), reads=[psB], writes=[kB]))
        for dvt in range(2):
            self.proj_fm(w, wB, 512 + 128 * dvt, 128, hT, hB, t0,
                         lambda ps, psB, dvt=dvt: S.op("act", lambda e: e.activation(out=rS[:, dvt, :], in_=ps, func=AF.Silu), reads=[psB], writes=[rSB]))
        for tt in range(NTT):
            bank, c0, bufs = self.small(2)
            for kt in range(KT):
                S.op("pe", lambda e, kt=kt, tt=tt, bank=bank, c0=c0: e.matmul(bank[:, c0:c0 + 256], lhsT=hT[:, kt, t0 + tt * 128:t0 + (tt + 1) * 128],
                                                                               rhs=w[:, kt, 256:512], start=(kt == 0), stop=(kt == KT - 1)),
                     reads=[wB, hB[kt]], writes=bufs)
            S.op("act", lambda e, tt=tt, bank=bank, c0=c0: e.activation(out=vtm[:, tt, :].bitcast(F32R), in_=bank[:, c0:c0 + 256], func=AF.Copy),
                 reads=bufs, writes=[vtmB])
        bank = self.next_bank()
        S.op("pe", lambda e, bank=bank: e.matmul(self.psum[bank][:, :MB], lhsT=self.gwa2[0:16, h * 128:(h + 1) * 128].bitcast(F32R),
                                                 rhs=adT[0:16, :].bitcast(F32R), start=True, stop=True),
             reads=[self.gwa2B, adB], writes=[self.psb[bank]])
        S.op("act", lambda e, bank=bank: e.activation(out=L[:], in_=self.psum[bank][:, :MB], func=AF.Exp, scale=-1.0, bias=self.mvd[:, h:h + 1]),
             reads=[self.psb[bank], self.mvdB], writes=[LB])
        S.op("act", lambda e: e.activation(out=L[:], in_=L[:], func=AF.Ln, bias=1.0, scale=1.0), reads=[LB], writes=[LB])
        S.op("dve", lambda e: e.tensor_tensor_scan(out=CL[:], data0=cmask, data1=L[:], initial=0.0, op0=ALU.mult, op1=ALU.add),
             reads=[LB, self.conB], writes=[CLB])
        CLv = CL[:].rearrange("p (c j) -> p c j", j=64)
        S.op("act", lambda e: e.activation(out=ET[:], in_=CLv[:, :, 63], func=AF.Exp, scale=-1.0 / 16.0), reads=[CLB], writes=[ETB])
        S.op("dve", lambda e: e.tensor_tensor(out=L[:].rearrange("p (c j) -> p c j", j=64), in0=CLv[:, :, 63:64].broadcast_to([128, NCK, 64]),
                                              in1=CLv, op=ALU.subtract), reads=[CLB], writes=[LB])
        S.op("act", lambda e: e.activation(out=L[:], in_=L[:], func=AF.Exp, scale=-1.0 / 16.0), reads=[LB], writes=[LB])
        S.op("dve", lambda e: e.tensor_tensor(out=KD[:], in0=kT[:], in1=L[:], op=ALU.mult), reads=[kB, LB], writes=[KDB])
        for tt in range(NTT):
            bank, c0, bufs = self.small(1)
            S.op("pe", lambda e, tt=tt, bank=bank, c0=c0: e.transpose(bank[:, c0:c0 + 128], KD[:, tt * 128:(tt + 1) * 128], self.c_ident()),
                 reads=[KDB, self.conB], writes=bufs)
            S.op("act", lambda e, tt=tt, bank=bank, c0=c0: e.activation(out=kdtm[:, tt, :].bitcast(F32R), in_=bank[:, c0:c0 + 128], func=AF.Copy),
                 reads=bufs, writes=[kdtmB])
        oT, oB = self.ct("oT", [128, 2, MB], F32)
        sq, sqB = self.ct("sq", [128, 2, MB], F32)
        Sg, SgB = self.Sg[h]
        obank = [self.next_bank(), self.next_bank()]
        for c in range(NCK):
            tt, rows = c // 2, (c % 2) * 64
            cur = self.gcur[h]
            new = 1 - cur
            self.gcur[h] = new
            bank, c0, bufs = self.small(2)
            S.op("pe", lambda e, tt=tt, rows=rows, bank=bank, c0=c0: e.matmul(bank[:, c0:c0 + 256], lhsT=kdtm[rows:rows + 64, tt, :].bitcast(F32R),
                                                                              rhs=vtm[rows:rows + 64, tt, :].bitcast(F32R), start=True, stop=True),
                 reads=[kdtmB, vtmB], writes=bufs)
            S.op("dve", lambda e, c=c, cur=cur, new=new, bank=bank, c0=c0: e.scalar_tensor_tensor(
                out=Sg[:, new, :].bitcast(F32R), in0=Sg[:, cur, :], scalar=ET[:, c:c + 1], in1=bank[:, c0:c0 + 256], op0=ALU.mult, op1=ALU.add),
                reads=bufs + [SgB, ETB], writes=[SgB])
            for dvt in range(2):
                S.op("pe", lambda e, c=c, new=new, dvt=dvt: e.matmul(self.psum[obank[dvt]][:, c * 64:(c + 1) * 64],
                                                                     lhsT=Sg[:, new, dvt * 128:(dvt + 1) * 128].bitcast(F32R),
                                                                     rhs=qT[:, c * 64:(c + 1) * 64].bitcast(F32R), start=True, stop=True),
                     reads=[SgB, qB], writes=[self.psb[obank[dvt]]])
        for dvt in range(2):
            S.op("act", lambda e, dvt=dvt: e.activation(out=oT[:, dvt, :], in_=self.psum[obank[dvt]][:, :MB], func=AF.Copy),
                 reads=[self.psb[obank[dvt]]], writes=[oB])
            S.op("act", lambda e, dvt=dvt: e.activation(out=sq[:, dvt, :].bitcast(F32R), in_=self.psum[obank[dvt]][:, :MB], func=AF.Square),
                 reads=[self.psb[obank[dvt]]], writes=[sqB])
        bank = self.next_bank()
        for dvt in range(2):
            S.op("pe", lambda e, dvt=dvt, bank=bank: e.matmul(self.psum[bank][:, :MB], lhsT=self.onesR[:], rhs=sq[:, dvt, :].bitcast(F32R),
                                                              start=(dvt == 0), stop=(dvt == 1)),
                 reads=[sqB, self.onesB], writes=[self.psb[bank]])
        S.op("act", lambda e, bank=bank: e.activation(out=L[:], in_=self.psum[bank][:, :MB], func=AF.Sqrt, scale=1.0 / 256.0, bias=self.eps_ap[:]),
             reads=[self.psb[bank], self.epsB], writes=[LB])
        S.op("dve", lambda e: e.reciprocal(out=L[:], in_=L[:]), reads=[LB], writes=[LB])
        ob, obB = self.ct("ob", [128, 2, MB], BF16)
        for dvt in range(2):
            S.op("dve", lambda e, dvt=dvt: e.scalar_tensor_tensor(out=oT[:, dvt, :], in0=oT[:, dvt, :], scalar=mv[:, MV["gla_gn"] + dvt:MV["gla_gn"] + dvt + 1],
                                                                  in1=L[:], op0=ALU.mult, op1=ALU.mult), reads=[oB, LB, self.mvB], writes=[oB])
            S.op("dve", lambda e, dvt=dvt: e.tensor_tensor(out=ob[:, dvt, :], in0=oT[:, dvt, :], in1=rS[:, dvt, :], op=ALU.mult),
                 reads=[oB, rSB], writes=[obB])
            ct = h * 2 + dvt
            S.op("sp", lambda e, dvt=dvt, ct=ct: e.dma_start(out=o_all[s, ct, :, t0:t0 + MB], in_=ob[:, dvt, :]), reads=[obB], dma=True)

    def rw_pair(self, s, pp, hT, hB, t0, twd, twdB, adS, adSB, sgd, sgdB, cmask, o_all):
        cfg, S, A = self.cfg, self.S, self.A
        KT, MB = cfg.KT, cfg.TH
        NCK, NTT = MB // 64, MB // 128
        mv = self.mv
        C0 = float(np.exp(-0.5))

        def col(name):
            return mv[:, MV[name] + pp:MV[name] + pp + 1]

        def fm(name, dt=F32):
            return self.ct(name, [128, MB], dt)

        w, wB = self.ct("wr", [128, KT, 384], BF16)
        S.op("pool", lambda e: e.dma_start(out=w[:].rearrange("p k c -> p (k c)"), in_=self.w_rw_d[pp]), writes=[wB], dma=True)
        rT, rB = fm("rT")
        kT, kB = fm("kT")
        vT, vB = fm("vT")
        self.proj_shift(w, wB, 0, 128, hT, hB, t0, 4 + pp * 3 + 0, MV["mu_rkv"] + pp * 3 + 0, rT[:], rB)
        self.proj_shift(w, wB, 128, 128, hT, hB, t0, 4 + pp * 3 + 1, MV["mu_rkv"] + pp * 3 + 1, kT[:], kB)
        self.proj_shift(w, wB, 256, 128, hT, hB, t0, 4 + pp * 3 + 2, MV["mu_rkv"] + pp * 3 + 2, vT[:], vB)
        SG, SGB = fm("SG")
        aT, aB = fm("aT")
        gT, gB = fm("gT")
        cs = slice(pp * 128, (pp + 1) * 128)
        bank = self.next_bank()
        S.op("pe", lambda e, bank=bank: e.matmul(self.psum[bank][:, :MB], lhsT=self.ww2[0:96, cs].bitcast(F32R), rhs=twd[0:96, :].bitcast(F32R),
                                                 start=True, stop=True), reads=[self.ww2B, twdB], writes=[self.psb[bank]])
        S.op("act", lambda e, bank=bank: e.activation(out=SG[:], in_=self.psum[bank][:, :MB], func=AF.Sigmoid, bias=col("w0"), scale=1.0),
             reads=[self.psb[bank], self.mvB], writes=[SGB])
        bank = self.next_bank()
        S.op("pe", lambda e, bank=bank: e.matmul(self.psum[bank][:, :MB], lhsT=self.wa2[0:96, cs].bitcast(F32R), rhs=adS[0:96, :].bitcast(F32R),
                                                 start=True, stop=True), reads=[self.wa2B, adSB], writes=[self.psb[bank]])
        S.op("act", lambda e, bank=bank: e.activation(out=aT[:], in_=self.psum[bank][:, :MB], func=AF.Sigmoid, bias=col("a0"), scale=1.0),
             reads=[self.psb[bank], self.mvB], writes=[aB])
        bank = self.next_bank()
        for g2 in range(2):
            S.op("pe", lambda e, bank=bank, g2=g2: e.matmul(self.psum[bank][:, :MB], lhsT=self.wg2[:, g2, cs].bitcast(F32R), rhs=sgd[:, g2, :].bitcast(F32R),
                                                            start=(g2 == 0), stop=(g2 == 1)), reads=[self.wg2B, sgdB], writes=[self.psb[bank]])
        S.op("act", lambda e, bank=bank: e.activation(out=gT[:], in_=self.psum[bank][:, :MB], func=AF.Copy), reads=[self.psb[bank]], writes=[gB])
        kk, kkB = fm("kk")
        t1, t1B = fm("t1")
        t1r, t1rB = fm("t1r")
        t2, t2B = fm("t2")
        S.op("dve", lambda e: e.tensor_scalar(out=kk[:], in0=kT[:], scalar1=col("k_k"), scalar2=None, op0=ALU.mult), reads=[kB, self.mvB], writes=[kkB])
        S.op("act", lambda e: e.activation(out=t1r[:].bitcast(F32R), in_=kk[:], func=AF.Square), reads=[kkB], writes=[t1rB])
        bank = self.next_bank()
        S.op("pe", lambda e, bank=bank: e.matmul(self.psum[bank][:, :MB], lhsT=self.c_bdonesR(), rhs=t1r[:].bitcast(F32R), start=True, stop=True),
             reads=[t1rB, self.conRB], writes=[self.psb[bank]])
        S.op("act", lambda e, bank=bank: e.activation(out=t2[:], in_=self.psum[bank][:, :MB], func=AF.Sqrt), reads=[self.psb[bank]], writes=[t2B])
        S.op("dve", lambda e: e.tensor_scalar(out=t2[:], in0=t2[:], scalar1=1e-12, scalar2=None, op0=ALU.max), reads=[t2B], writes=[t2B])
        S.op("dve", lambda e: e.reciprocal(out=t2[:], in_=t2[:]), reads=[t2B], writes=[t2B])
        S.op("dve", lambda e: e.tensor_tensor(out=kk[:], in0=kk[:], in1=t2[:], op=ALU.mult), reads=[kkB, t2B], writes=[kkB])
        kp, kpB = fm("kp")
        bT, bB = fm("bT")
        S.op("dve", lambda e: e.tensor_scalar(out=t2[:], in0=aT[:], scalar1=col("k_a"), scalar2=self.mvd[:, 4 + pp:5 + pp], op0=ALU.mult, op1=ALU.add),
             reads=[aB, self.mvB, self.mvdB], writes=[t2B])
        S.op("dve", lambda e: e.tensor_tensor(out=kp[:], in0=kT[:], in1=t2[:], op=ALU.mult), reads=[kB, t2B], writes=[kpB])
        S.op("dve", lambda e: e.tensor_tensor(out=bT[:], in0=kk[:], in1=aT[:], op=ALU.mult), reads=[kkB, aB], writes=[bB])
        BN, BNB = fm("BN")
        S.op("dve", lambda e: e.tensor_tensor(out=t2[:], in0=rT[:], in1=kp[:], op=ALU.mult), reads=[rB, kpB], writes=[t2B])
        S.op("dve", lambda e: e.tensor_scalar(out=t1r[:].bitcast(F32R), in0=t2[:], scalar1=col("r_k"), scalar2=None, op0=ALU.mult),
             reads=[t2B, self.mvB], writes=[t1rB])
        bank = self.next_bank()
        S.op("pe", lambda e, bank=bank: e.matmul(self.psum[bank][:, :MB], lhsT=self.c_bdonesR(), rhs=t1r[:].bitcast(F32R), start=True, stop=True),
             reads=[t1rB, self.conRB], writes=[self.psb[bank]])
        S.op("dve", lambda e, bank=bank: e.tensor_tensor(out=BN[:], in0=self.psum[bank][:, :MB], in1=vT[:], op=ALU.mult),
             reads=[self.psb[bank], vB], writes=[BNB])
        CL, CLB = fm("CL")
        S.op("dve", lambda e: e.tensor_tensor_scan(out=CL[:], data0=cmask, data1=SG[:], initial=0.0, op0=ALU.mult, op1=ALU.add),
             reads=[SGB, self.conB], writes=[CLB])
        CLv = CL[:].rearrange("p (c j) -> p c j", j=64)
        WC, WCB = self.ct("WC", [128, NCK], F32)
        S.op("act", lambda e: e.activation(out=WC[:], in_=CLv[:, :, 63], func=AF.Exp, scale=-C0), reads=[CLB], writes=[WCB])
        KR, KRB = self.ct("KR", [128, 2, MB], F32)
        kd, kdB = fm("kd")
        bd, bdB = fm("bd")
        kdp, kdpB = fm("kdp")
        nbdp, nbdpB = fm("nbdp")
        S.op("act", lambda e: e.activation(out=t1[:], in_=CL[:], func=AF.Exp, scale=-C0), reads=[CLB], writes=[t1B])
        S.op("dve", lambda e: e.tensor_tensor(out=KR[:, 1, :].bitcast(F32R), in0=rT[:], in1=t1[:], op=ALU.mult), reads=[rB, t1B], writes=[KRB])
        S.op("act", lambda e: e.activation(out=t2[:], in_=CL[:], func=AF.Exp, scale=C0), reads=[CLB], writes=[t2B])
        S.op("dve", lambda e: e.tensor_tensor(out=kd[:].bitcast(F32R), in0=kp[:], in1=t2[:], op=ALU.mult), reads=[kpB, t2B], writes=[kdB])
        S.op("dve", lambda e: e.tensor_tensor(out=bd[:].bitcast(F32R), in0=bT[:], in1=t2[:], op=ALU.mult), reads=[bB, t2B], writes=[bdB])
        S.op("dve", lambda e: e.tensor_tensor(out=t1[:], in0=CL[:], in1=SG[:], op=ALU.subtract), reads=[CLB, SGB], writes=[t1B])
        S.op("act", lambda e: e.activation(out=t1[:], in_=t1[:], func=AF.Exp, scale=-C0), reads=[t1B], writes=[t1B])
        S.op("dve", lambda e: e.tensor_tensor(out=KR[:, 0, :].bitcast(F32R), in0=kk[:], in1=t1[:], op=ALU.mult), reads=[kkB, t1B], writes=[KRB])
        S.op("dve", lambda e: e.tensor_tensor(out=t2[:].rearrange("p (c j) -> p c j", j=64), in0=CLv[:, :, 63:64].broadcast_to([128, NCK, 64]),
                                              in1=CLv, op=ALU.subtract), reads=[CLB], writes=[t2B])
        S.op("act", lambda e: e.activation(out=t2[:], in_=t2[:], func=AF.Exp, scale=-C0), reads=[t2B], writes=[t2B])
        S.op("dve", lambda e: e.tensor_tensor(out=kdp[:], in0=kp[:], in1=t2[:], op=ALU.mult), reads=[kpB, t2B], writes=[kdpB])
        S.op("dve", lambda e: e.scalar_tensor_tensor(out=nbdp[:], in0=bT[:], scalar=-1.0, in1=t2[:], op0=ALU.mult, op1=ALU.mult),
             reads=[bB, t2B], writes=[nbdpB])
        YT, YTB = fm("YT")
        DWall, DWallB = self.ct("DWall", [128, NCK, 64], F32)
        S.op("dve", lambda e: e.tensor_tensor(out=DWall[:].bitcast(F32R), in0=self.con[:, 896:960][:, None, :].broadcast_to([128, NCK, 64]),
                                              in1=WC[:, :, None].broadcast_to([128, NCK, 64]), op=ALU.mult),
             reads=[self.conB, WCB], writes=[DWallB])
        DGh = [self.ct("DGh%d" % i_, [64, NCK, 64], F32) for i_ in range(2)]
        for hh in range(2):
            pb = 64 * hh
            bank = self.next_bank()
            S.op("pe", lambda e, pb=pb, bank=bank: e.matmul(self.psum[bank][0:64, 0:NCK * 64], lhsT=self.conR[pb:pb + 64, pb:pb + 64].bitcast(F32R),
                                                            rhs=DWall[pb:pb + 64, :, :].bitcast(F32R), start=True, stop=True),
                 reads=[self.conRB, DWallB], writes=[self.psb[bank]])
            S.op("act", lambda e, hh=hh, bank=bank: e.activation(out=DGh[hh][0][:].rearrange("p c k -> p (c k)"), in_=self.psum[bank][0:64, 0:NCK * 64], func=AF.Copy),
                 reads=[self.psb[bank]], writes=[DGh[hh][1]])
        negMS = self.con[:, 256:384]
        M3m = self.con[:, 384:640]
        M24m = self.con[:, 640:896]
        identpair = self.con[:, 896:960]
        ysa = (self.psum[0], 0, [self.psb[0]])
        ysb = (self.psum[1], 0, [self.psb[1]])

        def rr(gens):
            gens = list(gens)
            while gens:
                for g_ in list(gens):
                    try:
                        next(g_)
                    except StopIteration:
                        gens.remove(g_)

        def tiles_of(tt):
            tmq, tmqB = self.ct("tmq%d" % (tt % 2), [128, 5, 128], F32)
            YL, YLB = self.ct("YL%d" % (tt % 2), [128, 128], F32)
            return tmq, tmqB, YL, YLB

        def hset(tt, hh):
            k = hh + 2 * (tt % 2)
            d = {}
            for nm, shp in (("Xn", [128, 2, 128]), ("Zn", [128, 2, 128]), ("ZN", [128, 256]), ("KKRK", [128, 256]), ("Pm", [128, 2, 128]),
                            ("R", [128, 128]), ("UK", [128, 128]), ("QT", [64, 128]), ("PT", [64, 2, 64]), ("Gsb", [64, 2, 64])):
                d[nm] = self.ct("%s%d" % (nm, k), shp, F32)
            return d

        def transposes(tt):
            ts = slice(tt * 128, (tt + 1) * 128)
            tmq, tmqB, YL, YLB = tiles_of(tt)
            for qi, (src, srcB, cc) in enumerate(((KR, KRB, 0), (kdp, kdpB, None), (nbdp, nbdpB, None), (vT, vB, None), (KR, KRB, 1))):
                bank, c0, bufs = self.small(1)
                sap = src[:, cc, ts] if cc is not None else src[:, ts]
                S.op("pe", lambda e, sap=sap, bank=bank, c0=c0: e.transpose(bank[:, c0:c0 + 128], sap, self.c_ident()),
                     reads=[srcB, self.conB], writes=bufs)
                S.op("act", lambda e, qi=qi, bank=bank, c0=c0: e.activation(out=tmq[:, qi, :].bitcast(F32R), in_=bank[:, c0:c0 + 128], func=AF.Copy),
                     reads=bufs, writes=[tmqB])

        def pre(tt, hh):
            ts = slice(tt * 128, (tt + 1) * 128)
            tmq, tmqB, YL, YLB = tiles_of(tt)
            H = hset(tt, hh)
            (Xn, XnB), (Zn, ZnB), (ZN, ZNB), (KKRK, KKRKB), (Pm, PmB) = H["Xn"], H["Zn"], H["ZN"], H["KKRK"], H["Pm"]
            (R, RB), (UK, UKB), (QT, QTB), (PT, PTB), (Gsb, GsbB) = H["R"], H["UK"], H["QT"], H["PT"], H["Gsb"]
            pb = 64 * hh
            kkw_h = KR[pb:pb + 64, 0, ts].bitcast(F32R)
            bd_h = bd[pb:pb + 64, ts].bitcast(F32R)
            kd_h = kd[pb:pb + 64, ts].bitcast(F32R)
            kr_h = KR[pb:pb + 64, :, ts].bitcast(F32R)
            b1 = self.small(1)
            S.op("pe", lambda e: e.matmul(b1[0][:, b1[1]:b1[1] + 128], lhsT=kkw_h, rhs=bd_h, start=True, stop=True), reads=[KRB, bdB], writes=b1[2])
            b2 = self.small(2)
            S.op("pe", lambda e: e.matmul(b2[0][:, b2[1]:b2[1] + 256], lhsT=bd_h, rhs=kr_h, start=True, stop=True), reads=[KRB, bdB], writes=b2[2])
            b3 = self.small(2)
            S.op("pe", lambda e: e.matmul(b3[0][:, b3[1]:b3[1] + 256], lhsT=kd_h, rhs=kr_h, start=True, stop=True), reads=[KRB, kdB], writes=b3[2])
            S.op("dve", lambda e: e.tensor_tensor(out=Xn[:, 0, :].bitcast(F32R), in0=b1[0][:, b1[1]:b1[1] + 128], in1=negMS, op=ALU.mult),
                 reads=b1[2] + [self.conB], writes=[XnB])
            S.op("dve", lambda e: e.tensor_tensor(out=ZN[:].bitcast(F32R), in0=b2[0][:, b2[1]:b2[1] + 256], in1=M24m, op=ALU.mult),
                 reads=b2[2] + [self.conB], writes=[ZNB])
            S.op("act", lambda e: e.activation(out=Zn[:, 0, :].bitcast(F32R), in_=ZN[:, 0:128], func=AF.Copy), reads=[ZNB], writes=[ZnB])
            S.op("dve", lambda e: e.tensor_tensor(out=Pm[:, 0, :].bitcast(F32R), in0=ZN[:, 0:128], in1=self.c_ident(), op=ALU.add),
                 reads=[ZNB, self.conB], writes=[PmB])
            S.op("dve", lambda e: e.tensor_tensor(out=KKRK[:].bitcast(F32R), in0=b3[0][:, b3[1]:b3[1] + 256], in1=M3m, op=ALU.mult),
                 reads=b3[2] + [self.conB], writes=[KKRKB])
            yield
            pc = 0
            for lv in range(1, 7):
                a, b_ = (lv - 1) % 2, lv % 2
                bx = bz = bp = None
                if lv <= 5:
                    bx = self.small(1)
                    S.op("pe", lambda e, a=a, bx=bx: e.matmul(bx[0][:, bx[1]:bx[1] + 128], lhsT=Zn[:, a, :].bitcast(F32R), rhs=Xn[:, a, :].bitcast(F32R),
                                                              start=True, stop=True), reads=[ZnB, XnB], writes=bx[2])
                if lv <= 4:
                    bz = self.small(1)
                    S.op("pe", lambda e, a=a, bz=bz: e.matmul(bz[0][:, bz[1]:bz[1] + 128], lhsT=Xn[:, a, :].bitcast(F32R), rhs=Zn[:, a, :].bitcast(F32R),
                                                              start=True, stop=True), reads=[ZnB, XnB], writes=bz[2])
                if lv >= 2:
                    bp = self.small(1)
                    S.op("pe", lambda e, a=a, pc=pc, bp=bp: e.matmul(bp[0][:, bp[1]:bp[1] + 128], lhsT=Xn[:, a, :].bitcast(F32R), rhs=Pm[:, pc, :].bitcast(F32R),
                                                                     start=True, stop=True), reads=[XnB, PmB], writes=bp[2])
                if bx is not None:
                    S.op("act", lambda e, b_=b_, bx=bx: e.activation(out=Xn[:, b_, :].bitcast(F32R), in_=bx[0][:, bx[1]:bx[1] + 128], func=AF.Copy),
                         reads=bx[2], writes=[XnB])
                if bz is not None:
                    S.op("act", lambda e, b_=b_, bz=bz: e.activation(out=Zn[:, b_, :].bitcast(F32R), in_=bz[0][:, bz[1]:bz[1] + 128], func=AF.Copy),
                         reads=bz[2], writes=[ZnB])
                if bp is not None:
                    S.op("dve", lambda e, pc=pc, bp=bp: e.tensor_tensor(out=Pm[:, 1 - pc, :].bitcast(F32R), in0=bp[0][:, bp[1]:bp[1] + 128],
                                                                        in1=Pm[:, pc, :], op=ALU.add), reads=bp[2] + [PmB], writes=[PmB])
                    pc = 1 - pc
                yield
            S.op("act", lambda e: e.activation(out=R[:, 0:64].bitcast(F32R), in_=tmq[:, 0, pb:pb + 64], func=AF.Copy), reads=[tmqB], writes=[RB])
            bw = self.small(1)
            S.op("pe", lambda e: e.matmul(bw[0][:, bw[1]:bw[1] + 64], lhsT=KKRK[:, 0:128].bitcast(F32R), rhs=tmq[:, 3, pb:pb + 64].bitcast(F32R),
                                          start=True, stop=True), reads=[KKRKB, tmqB], writes=bw[2])
            S.op("act", lambda e: e.activation(out=R[:, 64:128].bitcast(F32R), in_=bw[0][:, bw[1]:bw[1] + 64], func=AF.Copy), reads=bw[2], writes=[RB])
            yield
            bu = self.small(1)
            S.op("pe", lambda e: e.matmul(bu[0][:, bu[1]:bu[1] + 128], lhsT=Pm[:, pc, :].bitcast(F32R), rhs=R[:].bitcast(F32R), start=True, stop=True),
                 reads=[PmB, RB], writes=bu[2])
            S.op("act", lambda e: e.activation(out=UK[:].bitcast(F32R), in_=bu[0][:, bu[1]:bu[1] + 128], func=AF.Copy), reads=bu[2], writes=[UKB])
            yield
            by = self.small(1)
            if hh == 0:
                S.op("pe", lambda e: e.matmul(by[0][0:64, by[1]:by[1] + 128], lhsT=tmq[:, 3, 0:64].bitcast(F32R), rhs=KKRK[:, 128:256].bitcast(F32R),
                                              start=True, stop=False), reads=[tmqB, KKRKB], writes=by[2])
                S.op("pe", lambda e: e.matmul(by[0][0:64, by[1]:by[1] + 128], lhsT=UK[:, 64:128].bitcast(F32R), rhs=ZN[:, 128:256].bitcast(F32R),
                                              start=False, stop=True), reads=[UKB, ZNB], writes=by[2])
            else:
                S.op("pe", lambda e: e.matmul(by[0][:, by[1]:by[1] + 128], lhsT=tmq[:, 3, :].bitcast(F32R), rhs=KKRK[:, 128:256].bitcast(F32R),
                                              start=True, stop=False), reads=[tmqB, KKRKB], writes=by[2])
                S.op("pe", lambda e: e.matmul(by[0][:, by[1]:by[1] + 128], lhsT=UK[:].bitcast(F32R), rhs=ZN[:, 128:256].bitcast(F32R),
                                              start=False, stop=True), reads=[UKB, ZNB], writes=by[2])
            bq = self.small(1)
            S.op("pe", lambda e: e.matmul(bq[0][0:64, bq[1]:bq[1] + 128], lhsT=tmq[:, 4, pb:pb + 64].bitcast(F32R), rhs=self.c_identR(), start=True, stop=False),
                 reads=[tmqB, self.conRB], writes=bq[2])
            S.op("pe", lambda e: e.matmul(bq[0][0:64, bq[1]:bq[1] + 128], lhsT=UK[:, 0:64].bitcast(F32R), rhs=ZN[:, 128:256].bitcast(F32R), start=False, stop=True),
                 reads=[UKB, ZNB], writes=bq[2])
            if hh == 0:
                S.op("act", lambda e: e.activation(out=YL[0:64, :], in_=by[0][0:64, by[1]:by[1] + 128], func=AF.Copy), reads=by[2], writes=[YLB])
            else:
                S.op("act", lambda e: e.activation(out=YL[64:128, :], in_=by[0][64:128, by[1]:by[1] + 128], func=AF.Copy), reads=by[2], writes=[YLB])
            S.op("act", lambda e: e.activation(out=QT[:].bitcast(F32R), in_=bq[0][0:64, bq[1]:bq[1] + 128], func=AF.Copy), reads=bq[2], writes=[QTB])
            for c2 in range(2):
                ck = tt * 2 + c2
                rows = 64 * c2
                bpt = self.small(1)
                S.op("pe", lambda e, rows=rows, bpt=bpt: e.matmul(bpt[0][0:64, bpt[1]:bpt[1] + 64], lhsT=UK[rows:rows + 64, 0:64].bitcast(F32R),
                                                                  rhs=tmq[rows:rows + 64, 2, pb:pb + 64].bitcast(F32R), start=True, stop=True),
                     reads=[UKB, tmqB], writes=bpt[2])
                bg = self.small(1)
                S.op("pe", lambda e, rows=rows, bg=bg: e.matmul(bg[0][0:64, bg[1]:bg[1] + 64], lhsT=tmq[rows:rows + 64, 1, pb:pb + 64].bitcast(F32R),
                                                                rhs=tmq[rows:rows + 64, 3, pb:pb + 64].bitcast(F32R), start=True, stop=False),
                     reads=[tmqB], writes=bg[2])
                S.op("pe", lambda e, rows=rows, bg=bg: e.matmul(bg[0][0:64, bg[1]:bg[1] + 64], lhsT=tmq[rows:rows + 64, 2, pb:pb + 64].bitcast(F32R),
                                                                rhs=UK[rows:rows + 64, 64:128].bitcast(F32R), start=False, stop=True),
                     reads=[tmqB, UKB], writes=bg[2])
                S.op("dve", lambda e, c2=c2, ck=ck, bpt=bpt: e.tensor_tensor(out=PT[:, c2, :].bitcast(F32R), in0=bpt[0][0:64, bpt[1]:bpt[1] + 64],
                                                                             in1=DGh[hh][0][:, ck, :], op=ALU.add), reads=bpt[2] + [DGh[hh][1]], writes=[PTB])
                S.op("act", lambda e, c2=c2, bg=bg: e.activation(out=Gsb[:, c2, :], in_=bg[0][0:64, bg[1]:bg[1] + 64], func=AF.Copy), reads=bg[2], writes=[GsbB])
            yield

        def chain(tt, hh):
            H = hset(tt, hh)
            (QT, QTB), (PT, PTB), (Gsb, GsbB) = H["QT"], H["PT"], H["Gsb"]
            hd = pp * 2 + hh
            Tt, TtB = self.Tst[hd]
            tc0 = 64 * hh
            ysbank, ysc0, ysbufs = ysa if hh == 0 else ysb
            for c2 in range(2):
                cur = self.tcur[hd]
                new = 1 - cur
                self.tcur[hd] = new
                if hh == 0:
                    S.op("pe", lambda e, cur=cur, c2=c2: e.matmul(ysbank[0:64, ysc0 + 64 * c2:ysc0 + 64 * (c2 + 1)], lhsT=Tt[:, cur, 0:64].bitcast(F32R),
                                                                  rhs=QT[:, 64 * c2:64 * (c2 + 1)].bitcast(F32R), start=True, stop=True),
                         reads=[TtB, QTB], writes=ysbufs)
                else:
                    S.op("pe", lambda e, cur=cur, c2=c2: e.matmul(ysbank[:, ysc0 + 64 * c2:ysc0 + 64 * (c2 + 1)], lhsT=Tt[:, cur, :].bitcast(F32R),
                                                                  rhs=QT[:, 64 * c2:64 * (c2 + 1)].bitcast(F32R), start=True, stop=True),
                         reads=[TtB, QTB], writes=ysbufs)
                bc = self.small(1)
                S.op("pe", lambda e, c2=c2, cur=cur, bc=bc: e.matmul(bc[0][0:64, bc[1]:bc[1] + 64], lhsT=PT[:, c2, :].bitcast(F32R),
                                                                     rhs=Tt[:, cur, tc0:tc0 + 64].bitcast(F32R), start=True, stop=True),
                     reads=[PTB, TtB], writes=bc[2])
                S.op("dve", lambda e, c2=c2, new=new, bc=bc: e.tensor_tensor(out=Tt[:, new, tc0:tc0 + 64].bitcast(F32R), in0=bc[0][0:64, bc[1]:bc[1] + 64],
                                                                             in1=Gsb[:, c2, :], op=ALU.add), reads=bc[2] + [GsbB], writes=[TtB])
                yield

        for tt0 in range(0, NTT, 2):
            tts = [t_ for t_ in (tt0, tt0 + 1) if t_ < NTT]
            for tt in tts:
                transposes(tt)
            rr([pre(tt, hh) for tt in tts for hh in range(2)])
            for tt in tts:
                ts = slice(tt * 128, (tt + 1) * 128)
                tmq, tmqB, YL, YLB = tiles_of(tt)
                rr([chain(tt, 0), chain(tt, 1)])
                S.op("dve", lambda e, ts=ts, YL=YL: e.tensor_tensor(out=YT[0:64, ts].bitcast(F32R), in0=ysa[0][0:64, ysa[1]:ysa[1] + 128], in1=YL[0:64, :], op=ALU.add),
                     reads=ysa[2] + [YLB], writes=[YTB])
                S.op("dve", lambda e, ts=ts, YL=YL: e.tensor_tensor(out=YT[64:128, ts].bitcast(F32R), in0=ysb[0][64:128, ysb[1]:ysb[1] + 128], in1=YL[64:128, :], op=ALU.add),
                     reads=ysb[2] + [YLB], writes=[YTB])
        bank = self.next_bank()
        S.op("pe", lambda e, bank=bank: e.matmul(self.psum[bank][:, :MB], lhsT=self.c_bdonesR(), rhs=YT[:].bitcast(F32R), start=True, stop=True),
             reads=[YTB, self.conRB], writes=[self.psb[bank]])
        S.op("act", lambda e: e.activation(out=t1r[:].bitcast(F32R), in_=YT[:], func=AF.Square), reads=[YTB], writes=[t1rB])
        bank2 = self.next_bank()
        S.op("pe", lambda e, bank2=bank2: e.matmul(self.psum[bank2][:, :MB], lhsT=self.c_bdonesR(), rhs=t1r[:].bitcast(F32R), start=True, stop=True),
             reads=[t1rB, self.conRB], writes=[self.psb[bank2]])
        mean, meanB = fm("kk")
        S.op("act", lambda e, bank=bank: e.activation(out=mean[:], in_=self.psum[bank][:, :MB], func=AF.Copy, scale=1.0 / 64.0),
             reads=[self.psb[bank]], writes=[meanB])
        S.op("dve", lambda e: e.tensor_tensor(out=t2[:], in0=mean[:], in1=mean[:], op=ALU.mult), reads=[meanB], writes=[t2B])
        S.op("dve", lambda e, bank2=bank2: e.scalar_tensor_tensor(out=t2[:], in0=self.psum[bank2][:, :MB], scalar=1.0 / 64.0, in1=t2[:],
                                                                  op0=ALU.mult, op1=ALU.subtract), reads=[self.psb[bank2], t2B], writes=[t2B])
        S.op("act", lambda e: e.activation(out=t2[:], in_=t2[:], func=AF.Sqrt, bias=self.gneps[:], scale=1.0), reads=[t2B, self.gnepsB], writes=[t2B])
        S.op("dve", lambda e: e.reciprocal(out=t2[:], in_=t2[:]), reads=[t2B], writes=[t2B])
        S.op("dve", lambda e: e.tensor_tensor(out=t1[:], in0=YT[:], in1=mean[:], op=ALU.subtract), reads=[YTB, meanB], writes=[t1B])
        S.op("dve", lambda e: e.tensor_tensor(out=t1[:], in0=t1[:], in1=t2[:], op=ALU.mult), reads=[t1B, t2B], writes=[t1B])
        S.op("dve", lambda e: e.tensor_scalar(out=t1[:], in0=t1[:], scalar1=col("lnx_w"), scalar2=col("lnx_b"), op0=ALU.mult, op1=ALU.add),
             reads=[t1B, self.mvB], writes=[t1B])
        S.op("dve", lambda e: e.tensor_tensor(out=t1[:], in0=t1[:], in1=BN[:], op=ALU.add), reads=[t1B, BNB], writes=[t1B])
        ob, obB = self.ct("obr", [128, MB], BF16)
        S.op("dve", lambda e: e.tensor_tensor(out=ob[:], in0=t1[:], in1=gT[:], op=ALU.mult), reads=[t1B, gB], writes=[obB])
        S.op("sp", lambda e: e.dma_start(out=o_all[s, 8 + pp, :, t0:t0 + MB], in_=ob[:]), reads=[obB], dma=True)

    def build(self, mode="full"):
        try:
            return self._build(mode)
        except _Stop:
            self.S.emit()
            return self.nc

    def _build(self, mode="full"):
        cfg, S, A, nc = self.cfg, self.S, self.A, self.nc
        KT, T, TH, NH, FT, FQ, NQ = cfg.KT, cfg.T, cfg.TH, cfg.NH, cfg.FT, cfg.FQ, cfg.NQ
        DG = cfg.DG
        x_in = self.din("x_t", [4, 128, KT, T])
        vecs = self.din("vecs", [128, 4 * KT])
        wgu1 = self.din("wgu1", [FT, 128, 2 * KT * 128])
        wd1 = self.din("wd1", [NQ, KT // DG, 128, DG, FQ * 128])
        wgu2 = self.din("wgu2", [FT, 128, 2 * KT * 128])
        wd2 = self.din("wd2", [NQ, KT // DG, 128, DG, FQ * 128])
        self.w_lora_d = self.din("w_lora", [128, KT * 464])
        self.w_gla_d = self.din("w_gla", [4, 128, KT * 768])
        self.w_rw_d = self.din("w_rw", [8, 128, KT * 384])
        wc_d = self.din("w_c", [KT, 128, KT * 256 + 16 * 128])
        wo_d = self.din("w_o", [KT, 128, KT * 128])
        gb_d = self.din("gate_bt", [128, 2 * KT])
        out = self.dout("out_t", [128, KT, T])
        x1_all = nc.dram_tensor("x1_all", [4, 128, KT, T], F32, kind="Internal").ap()
        o_all = nc.dram_tensor("o_all", [4, 16, 128, T], BF16, kind="Internal").ap()
        if mode == "mixer":
            o_dbg = self.dout("o_dbg", [4, 16, 128, T])

        vT = A.alloc("vecs", [128, 4 * KT], F32)
        self.vecB = Buf("vecs")
        self.onesR = A.alloc("onesR", [128, 128], F32R)
        self.onesB = Buf("ones")
        self.eps_ap = A.alloc("eps", [128, 1], F32)
        self.epsB = Buf("eps")
        ones32 = A.alloc("ones32", [128, 128], F32)
        o32B = Buf("ones32")
        S.op("dve", lambda e: e.memset(ones32[:], 1.0), writes=[o32B])
        S.op("dve", lambda e: e.tensor_copy(out=self.onesR[:], in_=ones32[:]), reads=[o32B], writes=[self.onesB])
        S.op("dve", lambda e: e.memset(self.eps_ap[:], NORM_EPS), writes=[self.epsB])
        S.op("sp", lambda e: e.dma_start(out=vT[:], in_=vecs), writes=[self.vecB], dma=True)
        self.mixer_setup()
        self.barrier()
        self.stop("setup")

        for s in range(4):
            A.push()
            h2T = A.alloc("h2T", [128, KT, T], BF16)
            h2B = [Buf("h2_%d" % k) for k in range(KT)]
            A.push()
            xT = A.alloc("xT", [128, KT, T], F32)
            xB = [[Buf("x%d_%d" % (k, h)) for h in range(NH)] for k in range(KT)]
            for kt in range(KT):
                S.op("sp", lambda e, kt=kt, s=s: e.dma_start(out=xT[:, kt, :], in_=x_in[s, :, kt, :]), writes=[xB[kt][h] for h in range(NH)], dma=True)
            if mode == "dbg1":
                self.rmsnorm_fm(xT, xB, vT[:, KT:2 * KT], h2T, h2B)
                S.emit()
                return nc
            if mode != "mixer_only":
                self.ffn(xT, xB, vT[:, 0:KT], wgu1, wd1, hT=h2T, hB=h2B)
            for kt in range(KT):
                S.op("sp", lambda e, kt=kt, s=s: e.dma_start(out=x1_all[s, :, kt, :], in_=xT[:, kt, :]), reads=[xB[kt][h] for h in range(NH)], dma=True)
            self.rmsnorm_fm(xT, xB, vT[:, KT:2 * KT], h2T, h2B)
            A.pop()
            self.barrier()
            self.stop("ffn1")
            for hb in range(NH):
                self.mixer_block(s, hb, h2T, h2B, o_all)
            A.pop()
            self.barrier()

        A.pop()
        self.barrier()
        if mode == "mixer":
            A.push()
            for s in range(4):
                for ct in range(16):
                    tb, tbB = self.tl("dbg", [128, T], BF16)
                    tf, tfB = self.tl("dbgf", [128, T], F32)
                    S.op("sp", lambda e, s=s, ct=ct, tb=tb: e.dma_start(out=tb[:], in_=o_all[s, ct]), writes=[tbB], dma=True)
                    S.op("dve", lambda e, tb=tb, tf=tf: e.tensor_copy(out=tf[:], in_=tb[:]), reads=[tbB], writes=[tfB])
                    S.op("sp", lambda e, s=s, ct=ct, tf=tf: e.dma_start(out=o_dbg[s, ct], in_=tf[:]), reads=[tfB], dma=True)
                    self.A.cur -= 0
            A.pop()

        sel0 = MV["sel"]
        xT = A.alloc("xT", [128, KT, T], F32)
        xB = [[Buf("x%d_%d" % (k, h)) for h in range(NH)] for k in range(KT)]
        A.push()
        oT = A.alloc("oTc", [128, 16, T], BF16)
        oB = [Buf("oc%d" % k) for k in range(16)]
        A.push()
        tx = [self.tl("tx", [128, T], F32) for _ in range(2)]
        tb = [self.tl("tb", [128, T], BF16) for _ in range(2)]
        i = 0
        for kt in range(KT):
            for s in range(4):
                t_, tB_ = tx[i % 2]
                i += 1
                S.op("sp", lambda e, kt=kt, s=s, t_=t_: e.dma_start(out=t_[:], in_=x1_all[s, :, kt, :]), writes=[tB_], dma=True)
                if s == 0:
                    S.op("dve", lambda e, kt=kt, t_=t_: e.tensor_scalar(out=xT[:, kt, :], in0=t_[:], scalar1=self.mv[:, sel0:sel0 + 1], scalar2=None, op0=ALU.mult),
                         reads=[tB_, self.mvB], writes=[xB[kt][h] for h in range(NH)])
                else:
                    S.op("dve", lambda e, kt=kt, s=s, t_=t_: e.scalar_tensor_tensor(out=xT[:, kt, :], in0=t_[:], scalar=self.mv[:, sel0 + s:sel0 + s + 1],
                                                                                    in1=xT[:, kt, :], op0=ALU.mult, op1=ALU.add),
                         reads=[tB_, self.mvB] + [xB[kt][h] for h in range(NH)], writes=[xB[kt][h] for h in range(NH)])
        i = 0
        for ct in range(16):
            for s in range(4):
                t_, tB_ = tb[i % 2]
                i += 1
                S.op("sp", lambda e, ct=ct, s=s, t_=t_: e.dma_start(out=t_[:], in_=o_all[s, ct]), writes=[tB_], dma=True)
                if s == 0:
                    S.op("pool", lambda e, ct=ct, t_=t_: e.tensor_scalar(out=oT[:, ct, :], in0=t_[:], scalar1=self.mv[:, sel0:sel0 + 1], scalar2=None, op0=ALU.mult),
                         reads=[tB_, self.mvB], writes=[oB[ct]])
                else:
                    S.op("dve", lambda e, ct=ct, s=s, t_=t_: e.scalar_tensor_tensor(out=oT[:, ct, :], in0=t_[:], scalar=self.mv[:, sel0 + s:sel0 + s + 1],
                                                                                    in1=oT[:, ct, :], op0=ALU.mult, op1=ALU.add),
                         reads=[tB_, self.mvB, oB[ct]], writes=[oB[ct]])
        A.pop()
        self.barrier()
        hT = A.alloc("hTc", [128, KT, T], BF16)
        hB = [Buf("hc%d" % k) for k in range(KT)]
        self.rmsnorm_fm(xT, xB, vT[:, KT:2 * KT], hT, hB)
        mT = A.alloc("mT", [128, KT, T], BF16)
        mB = [Buf("m%d" % k) for k in range(KT)]
        gbt, gbtB = self.tl("gbt", [128, 2 * KT], F32)
        S.op("sp", lambda e: e.dma_start(out=gbt[:], in_=gb_d), writes=[gbtB], dma=True)
        WCW = KT * 256 + 16 * 128
        wcs = [self.tl("wc", [128, WCW], BF16) for _ in range(2)]
        sgt = [self.tl("sgt", [128, TH], F32) for _ in range(2)]
        m1t = [self.tl("m1t", [128, TH], F32) for _ in range(2)]
        for dt in range(KT):
            wc, wcB = wcs[dt % 2]
            S.op("pool", lambda e, dt=dt, wc=wc: e.dma_start(out=wc[:], in_=wc_d[dt]), writes=[wcB], dma=True)
            for h in range(NH):
                hs = slice(h * TH, (h + 1) * TH)
                pb = 4 * ((dt * NH + h) % 2)
                for br in range(2):
                    for kt in range(KT):
                        S.op("pe", lambda e, kt=kt, br=br, wc=wc, pb=pb, hs=hs: e.matmul(self.psum[pb + br][:, :TH], lhsT=wc[:, kt * 256 + br * 128:kt * 256 + (br + 1) * 128],
                                                                                        rhs=hT[:, kt, hs], start=(kt == 0), stop=(kt == KT - 1)),
                             reads=[wcB, hB[kt]], writes=[self.psb[pb + br]])
                    for k8 in range(8):
                        ct = br * 8 + k8
                        S.op("pe", lambda e, ct=ct, k8=k8, br=br, wc=wc, pb=pb, hs=hs: e.matmul(self.psum[pb + 2 + br][:, :TH],
                                                                                               lhsT=wc[:, KT * 256 + ct * 128:KT * 256 + (ct + 1) * 128],
                                                                                               rhs=oT[:, ct, hs], start=(k8 == 0), stop=(k8 == 7)),
                             reads=[wcB, oB[ct]], writes=[self.psb[pb + 2 + br]])
                sg0, sg0B = sgt[0]
                sg1, sg1B = sgt[1]
                m1, m1B = m1t[0]
                m2, m2B = m1t[1]
                S.op("act", lambda e, dt=dt, pb=pb: e.activation(out=sg0[:], in_=self.psum[pb][:, :TH], func=AF.Sigmoid, bias=gbt[:, dt:dt + 1], scale=1.0),
                     reads=[self.psb[pb], gbtB], writes=[sg0B])
                S.op("act", lambda e, dt=dt, pb=pb: e.activation(out=sg1[:], in_=self.psum[pb + 1][:, :TH], func=AF.Sigmoid, bias=gbt[:, KT + dt:KT + dt + 1], scale=1.0),
                     reads=[self.psb[pb + 1], gbtB], writes=[sg1B])
                S.op("dve", lambda e, pb=pb: e.tensor_tensor(out=m1[:], in0=sg0[:], in1=self.psum[pb + 2][:, :TH], op=ALU.mult),
                     reads=[sg0B, self.psb[pb + 2]], writes=[m1B])
                S.op("dve", lambda e, pb=pb: e.tensor_tensor(out=m2[:], in0=sg1[:], in1=self.psum[pb + 3][:, :TH], op=ALU.mult),
                     reads=[sg1B, self.psb[pb + 3]], writes=[m2B])
                S.op("dve", lambda e, dt=dt, hs=hs: e.tensor_tensor(out=mT[:, dt, hs], in0=m1[:], in1=m2[:], op=ALU.add),
                     reads=[m1B, m2B], writes=[mB[dt]])
        wos = [self.tl("wo", [128, KT * 128], BF16) for _ in range(2)]
        for dt in range(KT):
            wo, woB = wos[dt % 2]
            S.op("pool", lambda e, dt=dt, wo=wo: e.dma_start(out=wo[:], in_=wo_d[dt]), writes=[woB], dma=True)
            for h in range(NH):
                hs = slice(h * TH, (h + 1) * TH)
                bank = (dt * NH + h) % 8
                for kt in range(KT):
                    S.op("pe", lambda e, kt=kt, wo=wo, bank=bank, hs=hs: e.matmul(self.psum[bank][:, :TH], lhsT=wo[:, kt * 128:(kt + 1) * 128], rhs=mT[:, kt, hs],
                                                                                   start=(kt == 0), stop=(kt == KT - 1)),
                         reads=[woB, mB[kt]], writes=[self.psb[bank]])
                S.op("dve", lambda e, dt=dt, bank=bank, hs=hs, h=h: e.tensor_tensor(out=xT[:, dt, hs], in0=self.psum[bank][:, :TH], in1=xT[:, dt, hs], op=ALU.add),
                     reads=[self.psb[bank], xB[dt][h]], writes=[xB[dt][h]])
        A.pop()
        self.barrier()

        self.ffn(xT, xB, vT[:, 2 * KT:3 * KT], wgu2, wd2)

        A.push()
        oTf = A.alloc("oTf", [128, KT, T], F32)
        oBf = [Buf("of%d" % k) for k in range(KT)]
        self.rmsnorm_fm(xT, xB, vT[:, 3 * KT:4 * KT], oTf, oBf)
        for kt in range(KT):
            S.op("sp", lambda e, kt=kt: e.dma_start(out=out[:, kt, :], in_=oTf[:, kt, :]), reads=[oBf[kt]], dma=True)
        S.wait_all("sp", [t for t in S.rings["sp"].last if t is not None])
        S.q["sp"].ops.append(([], None, None))
        A.pop()
        S.emit()
        return nc


MV = {"gla_ba": 0, "gla_gn": 4, "mu_wd": 6, "mu_ad": 7, "mu_gd": 8, "mu_rkv": 10, "w0": 34, "a0": 42, "k_k": 50, "k_a": 58,
      "r_k": 66, "lnx_w": 74, "lnx_b": 82, "sel": 90}
MV_N = 94

GLA_QK, GLA_V, GLA_LORA = 512, 1024, 16
GLA_IN = 2 * GLA_QK + 2 * GLA_V + GLA_LORA
RWKV_W = 1024
RWKV_IN = 3 * RWKV_W + 96 + 96 + 256


def fm_vec(v, KT):
    return np.ascontiguousarray(v.reshape(KT, 128).T)


def prep_ffn(wg, wu, wd, cfg):
    KT, FT, FQ, NQ, DG = cfg.KT, cfg.FT, cfg.FQ, cfg.NQ, cfg.DG
    g = wg.reshape(KT, 128, FT, 128).transpose(2, 1, 0, 3)
    u = wu.reshape(KT, 128, FT, 128).transpose(2, 1, 0, 3)
    wgu = np.ascontiguousarray(np.stack([g, u], axis=2)).reshape(FT, 128, 2 * KT * 128)
    w = wd.reshape(NQ, FQ, 128, KT // DG, DG, 128).transpose(0, 3, 2, 4, 1, 5)
    wdt = np.ascontiguousarray(w).reshape(NQ, KT // DG, 128, DG, FQ * 128)
    return wgu, wdt


def pkc(wcols, KT):
    C = wcols.shape[1]
    return np.ascontiguousarray(wcols.reshape(KT, 128, C).transpose(1, 0, 2)).reshape(128, KT * C)


def col128(v, n):
    out = np.zeros((128, n), np.float32)
    L = v.shape[0]
    if n == 1:
        out[:L, 0] = v
    else:
        out[:, :] = v.reshape(n, 128).T
    return out


def make_consts(MB):
    ident = np.eye(128, dtype=np.float32)
    bd = np.zeros((128, 128), np.float32)
    bd[:64, :64] = 1
    bd[64:, 64:] = 1
    p = np.arange(128)[:, None]
    f = np.arange(128)[None, :]
    same = (p // 64) == (f // 64)
    negMS = np.where(same & (p > f), -1.0, 0.0).astype(np.float32)
    MST = np.where(same & (f > p), 1.0, 0.0).astype(np.float32)
    MIT = np.where(same & (f >= p), 1.0, 0.0).astype(np.float32)
    identpair = np.concatenate([np.eye(64, dtype=np.float32)] * 2, axis=0)
    cmask = np.ones((128, MB), np.float32)
    cmask[:, 0::64] = 0
    return np.ascontiguousarray(np.concatenate([ident, bd, negMS, MST, MIT, -MST, -MIT, identpair, cmask], axis=1))


def make_in_maps(inputs, cfg, n_cores=8):
    D, T, KT = cfg.D, cfg.T, cfg.KT
    f32 = lambda k: np.asarray(inputs[k], dtype=np.float32)
    x = f32("x")
    vecs = np.concatenate([fm_vec(f32(k).reshape(-1), KT) for k in ("ffn1_norm", "mix_norm", "ffn2_norm", "final_norm")], axis=1)
    wgu1, wd1 = prep_ffn(f32("ffn1_wg")[0], f32("ffn1_wu")[0], f32("ffn1_wd")[0], cfg)
    wgu2, wd2 = prep_ffn(f32("ffn2_wg")[0], f32("ffn2_wu")[0], f32("ffn2_wd")[0], cfg)
    w_in = f32("w_in")[0]
    g0, r0, gt0 = 0, GLA_IN, GLA_IN + RWKV_IN
    lora_cols = np.concatenate([w_in[:, 3072:3088], w_in[:, r0 + 3072:r0 + 3168], w_in[:, r0 + 3168:r0 + 3264], w_in[:, r0 + 3264:r0 + 3520]], axis=1)
    w_lora = pkc(lora_cols, KT)
    w_gla = np.stack([pkc(np.concatenate([w_in[:, h * 128:(h + 1) * 128], w_in[:, 512 + h * 128:512 + (h + 1) * 128],
                                          w_in[:, 1024 + h * 256:1024 + (h + 1) * 256], w_in[:, 2048 + h * 256:2048 + (h + 1) * 256]], axis=1), KT)
                      for h in range(4)])
    w_rw = np.stack([pkc(np.concatenate([w_in[:, r0 + m * 1024 + pp * 128:r0 + m * 1024 + (pp + 1) * 128] for m in range(3)], axis=1), KT)
                     for pp in range(8)])
    wb = f32("w_branch")[0]
    wout = f32("w_out")[0]
    w_c = []
    w_o = []
    for dt in range(KT):
        gcols = np.concatenate([w_in[:, gt0 + dt * 128:gt0 + (dt + 1) * 128], w_in[:, gt0 + D + dt * 128:gt0 + D + (dt + 1) * 128]], axis=1)
        bcols = wb[:, dt * 128:(dt + 1) * 128]
        w_c.append(np.concatenate([pkc(gcols, KT), pkc(bcols, 16)], axis=1))
        w_o.append(pkc(wout[:, dt * 128:(dt + 1) * 128], KT))
    w_c = np.ascontiguousarray(np.stack(w_c))
    w_o = np.ascontiguousarray(np.stack(w_o))
    gb = f32("gate_b")[0]
    gate_bt = np.ascontiguousarray(np.concatenate([fm_vec(gb[:D], KT), fm_vec(gb[D:], KT)], axis=1))
    mu = f32("rwkv_mu")[0]
    mvb = np.zeros((128, MV_N), np.float32)
    mvb[:, MV["gla_ba"]:MV["gla_ba"] + 4] = col128(f32("gla_b_a")[0], 4)
    mvb[:, MV["gla_gn"]:MV["gla_gn"] + 2] = col128(f32("gla_gn_w")[0], 2)
    mvb[:, MV["mu_wd"]:MV["mu_wd"] + 1] = col128(mu[3072:3168], 1)
    mvb[:, MV["mu_ad"]:MV["mu_ad"] + 1] = col128(mu[3168:3264], 1)
    mvb[:, MV["mu_gd"]:MV["mu_gd"] + 2] = col128(mu[3264:3520], 2)
    for pp in range(8):
        for m in range(3):
            mvb[:, MV["mu_rkv"] + pp * 3 + m] = mu[m * 1024 + pp * 128:m * 1024 + (pp + 1) * 128]
    for nm, key in (("w0", "rwkv_w0"), ("a0", "rwkv_a0"), ("k_k", "rwkv_k_k"), ("k_a", "rwkv_k_a"), ("r_k", "rwkv_r_k"),
                    ("lnx_w", "rwkv_lnx_w"), ("lnx_b", "rwkv_lnx_b")):
        mvb[:, MV[nm]:MV[nm] + 8] = col128(f32(key)[0].reshape(-1), 8)
    consts = make_consts(cfg.TH)
    gla_wa2 = np.ascontiguousarray(f32("gla_w_a2")[0])
    rw_ww2 = np.ascontiguousarray(f32("rwkv_w_w2")[0])
    rw_wa2 = np.ascontiguousarray(f32("rwkv_w_a2")[0])
    rw_wg2 = np.ascontiguousarray(f32("rwkv_w_g2")[0].reshape(2, 128, 1024).transpose(1, 0, 2))
    maps = []
    xb_cache = {}
    for c in range(n_cores):
        b, j = c // 4, c % 4
        if b not in xb_cache:
            xs = x[b].reshape(4, T, KT, 128)
            xb_cache[b] = np.ascontiguousarray(xs.transpose(0, 3, 2, 1))
        mvc = mvb.copy()
        mvc[:, MV["sel"] + j] = 1.0
        maps.append({"x_t": xb_cache[b], "vecs": np.ascontiguousarray(vecs), "wgu1": wgu1, "wd1": wd1, "wgu2": wgu2, "wd2": wd2,
                     "w_lora": w_lora, "w_gla": w_gla, "w_rw": w_rw, "w_c": w_c, "w_o": w_o, "gate_bt": gate_bt, "mvecs": mvc,
                     "consts": consts, "gla_wa2": gla_wa2, "rw_ww2": rw_ww2, "rw_wa2": rw_wa2, "rw_wg2": rw_wg2})
    return maps


def assemble(results, cfg, n_cores=8):
    D, T, KT = cfg.D, cfg.T, cfg.KT
    out = np.zeros((n_cores // 4, 4 * T, D), np.float32)
    for c in range(n_cores):
        b, j = c // 4, c % 4
        o = np.asarray(results[c]["out_t"]).reshape(128, KT, T)
        out[b, j * T:(j + 1) * T, :] = o.transpose(2, 1, 0).reshape(T, D)
    return out


_CACHE = {}


def kernel(**inputs):
    cfg = Cfg()
    if "nc" not in _CACHE:
        _CACHE["nc"] = Prog(cfg).build()
    nc = _CACHE["nc"]
    maps = make_in_maps(inputs, cfg)
    res = run_bass_kernel_spmd(nc, maps, core_ids=list(range(8)))
    return assemble(res.results, cfg)
```

```python
import numpy as np
import concourse.bass as bass
import concourse.mybir as mybir
from concourse.bass_utils import run_bass_kernel_spmd

F32 = mybir.dt.float32
F32R = mybir.dt.float32r
BF16 = mybir.dt.bfloat16
U8 = mybir.dt.uint8
I32 = mybir.dt.int32
AF = mybir.ActivationFunctionType
ALU = mybir.AluOpType
AX = mybir.AxisListType

NORM_EPS = 1e-6
GN_EPS = 64e-5


class Cfg:
    def __init__(self, D=2048, FF=5632, T=1024, NQ=4, TH=512, stages="all"):
        self.D, self.FF, self.T, self.NQ = D, FF, T, NQ
        self.KT = D // 128
        self.FT = FF // 128
        self.FQ = self.FT // NQ
        self.TH = min(TH, T)
        self.NH = T // self.TH
        self.SEQ = 4 * T
        self.DG = min(2, self.KT)
        self.stages = stages


class Buf:
    __slots__ = ("name", "w", "r", "excl")

    def __init__(self, name, excl=False):
        self.name, self.w, self.r, self.excl = name, None, {}, excl


class Q:
    def __init__(self, name, sem, kind):
        self.name, self.sem, self.kind = name, sem, kind
        self.ops = []
        self.cnt = 0
        self.known = {}


class DmaRing:
    def __init__(self, name, sems):
        self.name, self.sems = name, sems
        self.n = 0
        self.last = [None] * len(sems)


class _Rec:
    def __getattr__(self, name):
        return lambda *a, **k: (name, a, k)


_REC = _Rec()


class Sched:
    def __init__(self, nc):
        self.nc = nc
        self.q = {}
        for nm, kind in (("pe", "pe"), ("act", "act"), ("dve", "dve"), ("pool", "pool"), ("sp", "sp")):
            self.q[nm] = Q(nm, nc.alloc_semaphore("sem_" + nm), kind)
        self.rings = {
            "sp": DmaRing("sp", [nc.alloc_semaphore("dsp%d" % i) for i in range(8)]),
            "pool": DmaRing("pool", [nc.alloc_semaphore("dpl%d" % i) for i in range(12)]),
            "act": DmaRing("act", [nc.alloc_semaphore("dac%d" % i) for i in range(6)]),
        }

    @staticmethod
    def _key(sem):
        return sem.num

    def _need(self, q, waits, tok):
        if tok is None:
            return
        sem, val = tok
        k = sem.num
        if q.known.get(k, 0) >= val:
            return
        if k not in waits or waits[k][1] < val:
            waits[k] = (sem, val)

    def op(self, qn, fn, reads=(), writes=(), dma=False):
        q = self.q[qn]
        waits = {}
        excl = [b for b in reads if b.excl]
        if excl:
            reads = [b for b in reads if not b.excl]
            writes = list(writes) + excl
        for b in reads:
            self._need(q, waits, b.w)
        for b in writes:
            self._need(q, waits, b.w)
            for t in b.r.values():
                self._need(q, waits, t)
        if dma:
            ring = self.rings[qn]
            slot = ring.n % len(ring.sems)
            self._need(q, waits, ring.last[slot])
            tok = (ring.sems[slot], 16 * (ring.n // len(ring.sems) + 1))
            ring.last[slot] = tok
            ring.n += 1
            inc = (tok[0], 16)
        else:
            if q.kind == "pe":
                waits.pop(q.sem.num, None)
            q.cnt += 1
            tok = (q.sem, q.cnt)
            inc = (q.sem, 1)
        wl = list(waits.values())
        for sem, val in wl:
            q.known[sem.num] = max(q.known.get(sem.num, 0), val)
        q.ops.append((wl, fn(_REC), inc))
        for b in reads:
            k = tok[0].num
            if k not in b.r or b.r[k][1] < tok[1]:
                b.r[k] = tok
        for b in writes:
            b.w = tok
            b.r = {}
        return tok

    def wait_all(self, qn, toks):
        q = self.q[qn]
        waits = {}
        for t in toks:
            self._need(q, waits, t)
        wl = list(waits.values())
        for sem, val in wl:
            q.known[sem.num] = max(q.known.get(sem.num, 0), val)
        if wl:
            q.ops.append((wl, None, None))

    def emit(self):
        nc = self.nc
        with nc.Block() as block:
            def run(q):
                def body(e):
                    for wl, fn, inc in q.ops:
                        for sem, val in wl:
                            e.wait_ge(sem, val)
                        if fn is not None:
                            ins = getattr(e, fn[0])(*fn[1], **fn[2])
                            ins.then_inc(inc[0], inc[1])
                return body
            block.tensor(run(self.q["pe"]))
            block.scalar(run(self.q["act"]))
            block.vector(run(self.q["dve"]))
            block.gpsimd(run(self.q["pool"]))
            block.sync(run(self.q["sp"]))


class Arena:
    def __init__(self, nc, lo, hi):
        self.nc, self.lo, self.hi = nc, lo, hi
        self.cur = lo
        self.n = 0
        self.marks = []

    def alloc(self, name, shape, dtype):
        esz = {F32: 4, F32R: 4, BF16: 2, U8: 1, I32: 4}[dtype]
        nbytes = esz
        for s in shape[1:]:
            nbytes *= s
        off = (self.cur + 31) // 32 * 32
        assert off + nbytes <= self.hi, ("SBUF arena overflow", name, off, nbytes, self.hi)
        self.cur = off + nbytes
        self.n += 1
        return self.nc.alloc_sbuf_tensor_at("%s_%d" % (name, self.n), list(shape), dtype, offset=off)

    def push(self):
        self.marks.append(self.cur)

    def pop(self):
        self.cur = self.marks.pop()


class _Stop(Exception):
    pass


class Prog:
    def stop(self, label):
        if getattr(self.cfg, "stop_at", None) == label:
            raise _Stop()

    def __init__(self, cfg):
        self.cfg = cfg
        nc = bass.Bass("TRN2", target_bir_lowering=False)
        self.nc = nc
        self.S = Sched(nc)
        lo = (nc.sbuf_base + 31) // 32 * 32
        hi = nc.sbuf_top // 32 * 32
        self.fence = nc.alloc_sbuf_tensor("arena_fence", [128, hi - lo], U8)
        self.A = Arena(nc, lo, hi)
        self.psum = [nc.alloc_psum_tensor("ps%d" % i, [128, 512], F32) for i in range(8)]
        self.psb = [Buf("ps%d" % i, excl=True) for i in range(8)]
        self.dram = {}

    def din(self, name, shape, dtype=F32):
        t = self.nc.dram_tensor(name, list(shape), dtype, kind="ExternalInput")
        self.dram[name] = t
        return t.ap()

    def dout(self, name, shape, dtype=F32):
        t = self.nc.dram_tensor(name, list(shape), dtype, kind="ExternalOutput")
        self.dram[name] = t
        return t.ap()

    def barrier(self):
        S = self.S
        toks = []
        for q in S.q.values():
            if q.cnt:
                toks.append((q.sem, q.cnt))
        for r in S.rings.values():
            for t in r.last:
                if t is not None:
                    toks.append(t)
        for qn in S.q:
            S.wait_all(qn, toks)

    def rmsnorm_fm(self, xT, xB, gT, outT, outB, out_dtype_note=None):
        cfg, S, A = self.cfg, self.S, self.A
        KT, T, TH, NH = cfg.KT, cfg.T, cfg.TH, cfg.NH
        A.push()
        sq = [A.alloc("sq", [128, T], F32R) for _ in range(2)]
        sqB = [Buf("sq0"), Buf("sq1")]
        rstd = A.alloc("rstd", [128, T], F32)
        rstdB = Buf("rstd")
        ones = self.onesR
        for kt in range(KT):
            s = kt % 2
            S.op("act", lambda e, kt=kt, s=s: e.activation(out=sq[s][:], in_=xT[:, kt, :], func=AF.Square),
                 reads=[xB[kt][h] for h in range(NH)], writes=[sqB[s]])
            for h in range(NH):
                S.op("pe", lambda e, kt=kt, s=s, h=h: e.matmul(self.psum[h][:, :TH], lhsT=ones[:], rhs=sq[s][:, h * TH:(h + 1) * TH],
                                                                start=(kt == 0), stop=(kt == KT - 1)),
                     reads=[sqB[s], self.onesB], writes=[self.psb[h]])
        for h in range(NH):
            S.op("act", lambda e, h=h: e.activation(out=rstd[:, h * TH:(h + 1) * TH], in_=self.psum[h][:, :TH], func=AF.Sqrt,
                                                    scale=1.0 / cfg.D, bias=self.eps_ap[:]),
                 reads=[self.psb[h], self.epsB], writes=[rstdB])
        S.op("dve", lambda e: e.reciprocal(out=rstd[:], in_=rstd[:]), reads=[rstdB], writes=[rstdB])
        for kt in range(KT):
            S.op("dve", lambda e, kt=kt: e.scalar_tensor_tensor(out=outT[:, kt, :], in0=xT[:, kt, :], scalar=gT[:, kt:kt + 1],
                                                                in1=rstd[:], op0=ALU.mult, op1=ALU.mult),
                 reads=[xB[kt][h] for h in range(NH)] + [rstdB, self.vecB], writes=[outB[kt]])
        A.pop()
        self.barrier()

    def ffn(self, xT, xB, gT, wgu, wd, hT=None, hB=None):
        cfg, S, A = self.cfg, self.S, self.A
        KT, T, TH, NH, FQ, NQ = cfg.KT, cfg.T, cfg.TH, cfg.NH, cfg.FQ, cfg.NQ
        A.push()
        if hT is None:
            hT = A.alloc("hT", [128, KT, T], BF16)
            hB = [Buf("h%d" % k) for k in range(KT)]
        self.rmsnorm_fm(xT, xB, gT, hT, hB)
        aT = A.alloc("aT", [128, FQ, T], BF16)
        aB = [[Buf("a") for _ in range(NH)] for _ in range(FQ)]
        NW = 2
        wsl = [A.alloc("wgu", [128, 2 * KT * 128], BF16) for _ in range(NW)]
        wslB = [Buf("wgu%d" % i) for i in range(NW)]
        ND = 2
        DG = cfg.DG
        dsl = [A.alloc("wd", [128, DG, FQ * 128], BF16) for _ in range(ND)]
        dslB = [Buf("wd%d" % i) for i in range(ND)]
        sg = [A.alloc("sg", [128, TH], BF16) for _ in range(2)]
        sgB = [Buf("sg0"), Buf("sg1")]
        wi = 0
        di = 0
        ei = 0
        for q in range(NQ):
            for f in range(FQ):
                fc = q * FQ + f
                s = wi % NW
                wi += 1
                S.op("pool", lambda e, s=s, fc=fc: e.dma_start(out=wsl[s][:], in_=wgu[fc]), writes=[wslB[s]], dma=True)
                pb = 4 * (fc % 2)
                for m in range(2):
                    for h in range(NH):
                        bank = pb + 2 * m + h
                        for kt in range(KT):
                            S.op("pe", lambda e, s=s, m=m, h=h, kt=kt, bank=bank: e.matmul(
                                self.psum[bank][:, :TH], lhsT=wsl[s][:, (m * KT + kt) * 128:(m * KT + kt + 1) * 128],
                                rhs=hT[:, kt, h * TH:(h + 1) * TH], start=(kt == 0), stop=(kt == KT - 1)),
                                reads=[wslB[s], hB[kt]], writes=[self.psb[bank]])
                for h in range(NH):
                    es = ei % 2
                    ei += 1
                    S.op("act", lambda e, es=es, h=h, pb=pb: e.activation(out=sg[es][:], in_=self.psum[pb + h][:, :TH], func=AF.Silu),
                         reads=[self.psb[pb + h]], writes=[sgB[es]])
                    S.op("dve", lambda e, es=es, h=h, pb=pb, f=f: e.tensor_tensor(out=aT[:, f, h * TH:(h + 1) * TH], in0=sg[es][:],
                                                                               in1=self.psum[pb + 2 + h][:, :TH], op=ALU.mult),
                         reads=[sgB[es], self.psb[pb + 2 + h]], writes=[aB[f][h]])
            for dg in range(KT // DG):
                s = di % ND
                di += 1
                S.op("pool", lambda e, s=s, q=q, dg=dg: e.dma_start(out=dsl[s][:], in_=wd[q, dg]), writes=[dslB[s]], dma=True)
                for d in range(DG):
                    dt = dg * DG + d
                    for h in range(NH):
                        bank = (dt * NH + h) % 8
                        for f in range(FQ):
                            S.op("pe", lambda e, s=s, d=d, f=f, h=h, bank=bank: e.matmul(
                                self.psum[bank][:, :TH], lhsT=dsl[s][:, d, f * 128:(f + 1) * 128],
                                rhs=aT[:, f, h * TH:(h + 1) * TH], start=(f == 0), stop=(f == FQ - 1)),
                                reads=[dslB[s], aB[f][h]], writes=[self.psb[bank]])
                        S.op("dve", lambda e, dt=dt, h=h, bank=bank: e.scalar_tensor_tensor(
                            out=xT[:, dt, h * TH:(h + 1) * TH], in0=self.psum[bank][:, :TH], scalar=0.5,
                            in1=xT[:, dt, h * TH:(h + 1) * TH], op0=ALU.mult, op1=ALU.add),
                            reads=[self.psb[bank], xB[dt][h]], writes=[xB[dt][h]])
        A.pop()
        self.barrier()

    def tl(self, name, shape, dtype):
        return self.A.alloc(name, shape, dtype), Buf(name)

    def ct(self, key, shape, dtype):
        if key not in self._cache:
            self._cache[key] = self.tl(key, shape, dtype)
        return self._cache[key]

    def next_bank(self):
        b = self._bank % 4
        self._bank += 1
        return b

    def small(self, n=1):
        i = self._small
        self._small += 1
        bank = 2 + i % 6
        c0 = ((i // 6) % 2) * 256 if n == 2 else ((i // 6) % 4) * 128
        return self.psum[bank], c0, [self.psb[bank]]

    def mixer_setup(self):
        cfg, S, A = self.cfg, self.S, self.A
        T, MB = cfg.T, cfg.TH
        self._bank = 0
        self._small = 0
        self.pssb = [Buf("pss%d" % i) for i in range(16)]
        NMV = MV_N
        mv_d = self.din("mvecs", [128, NMV])
        NCON = 960 + MB
        con_d = self.din("consts", [128, NCON])
        gwa2_d = self.din("gla_wa2", [16, 512])
        ww2_d = self.din("rw_ww2", [96, 1024])
        wa2_d = self.din("rw_wa2", [96, 1024])
        wg2_d = self.din("rw_wg2", [128, 2, 1024])
        self.mv, self.mvB = self.tl("mv", [128, NMV], F32)
        self.gneps, self.gnepsB = self.tl("gneps", [128, 1], F32)
        A.push()
        self.con, self.conB = self.tl("con", [128, NCON], F32)
        self.conR, self.conRB = self.tl("conR", [128, 256], F32)
        self.mvd, self.mvdB = self.tl("mvd", [128, 12], F32)
        self.gwa2, self.gwa2B = self.tl("gwa2", [16, 512], F32)
        self.ww2, self.ww2B = self.tl("ww2", [96, 1024], F32)
        self.wa2, self.wa2B = self.tl("wa2", [96, 1024], F32)
        self.wg2, self.wg2B = self.tl("wg2", [128, 2, 1024], F32)
        self.Sg = [self.tl("Sg%d" % h, [128, 2, 256], F32) for h in range(4)]
        self.Tst = [self.tl("Tst%d" % h, [64, 2, 128], F32) for h in range(16)]
        self.carry, self.carryB = self.tl("carry", [128, 28], F32)
        stg, stgB = self.tl("stg", [128, 2, 1024], F32)
        S.op("sp", lambda e: e.dma_start(out=self.mv[:], in_=mv_d), writes=[self.mvB], dma=True)
        S.op("sp", lambda e: e.dma_start(out=self.con[:], in_=con_d), writes=[self.conB], dma=True)
        S.op("dve", lambda e: e.tensor_copy(out=self.conR[:].bitcast(F32R), in_=self.con[:, 0:256]), reads=[self.conB], writes=[self.conRB])
        S.op("dve", lambda e: e.tensor_scalar(out=self.mvd[:, 0:4], in0=self.mv[:, MV["gla_ba"]:MV["gla_ba"] + 4], scalar1=-1.0, scalar2=None,
                                              op0=ALU.mult), reads=[self.mvB], writes=[self.mvdB])
        S.op("dve", lambda e: e.tensor_scalar(out=self.mvd[:, 4:12], in0=self.mv[:, MV["k_a"]:MV["k_a"] + 8], scalar1=-1.0, scalar2=1.0,
                                              op0=ALU.mult, op1=ALU.add), reads=[self.mvB], writes=[self.mvdB])
        S.op("dve", lambda e: e.memset(self.gneps[:], GN_EPS), writes=[self.gnepsB])
        for (dst, dB, src, np_, shp) in ((self.gwa2, self.gwa2B, gwa2_d, 16, None), (self.ww2, self.ww2B, ww2_d, 96, None),
                                         (self.wa2, self.wa2B, wa2_d, 96, None)):
            n = 512 if np_ == 16 else 1024
            S.op("sp", lambda e, src=src, np_=np_, n=n: e.dma_start(out=stg[:np_, 0, :n], in_=src), writes=[stgB], dma=True)
            S.op("dve", lambda e, dst=dst, np_=np_, n=n: e.tensor_copy(out=dst[:].bitcast(F32R), in_=stg[:np_, 0, :n]), reads=[stgB], writes=[dB])
        S.op("sp", lambda e: e.dma_start(out=stg[:], in_=wg2_d), writes=[stgB], dma=True)
        S.op("dve", lambda e: e.tensor_copy(out=self.wg2[:].bitcast(F32R), in_=stg[:]), reads=[stgB], writes=[self.wg2B])
        S.op("pool", lambda e: e.memset(stg[:, 0, 0:512], 0.0), writes=[stgB])
        for h in range(4):
            S.op("dve", lambda e, h=h: e.tensor_copy(out=self.Sg[h][0][:].rearrange("p a b -> p (a b)").bitcast(F32R), in_=stg[:, 0, 0:512]),
                 reads=[stgB], writes=[self.Sg[h][1]])
        for h in range(16):
            S.op("dve", lambda e, h=h: e.tensor_copy(out=self.Tst[h][0][:].rearrange("p a b -> p (a b)").bitcast(F32R), in_=stg[0:64, 0, 0:256]),
                 reads=[stgB], writes=[self.Tst[h][1]])
        S.op("pool", lambda e: e.memset(self.carry[:], 0.0), writes=[self.carryB])
        self.gcur = [0] * 4
        self.tcur = [0] * 16

    def c_ident(self):
        return self.con[:, 0:128]

    def c_identR(self):
        return self.conR[:, 0:128].bitcast(F32R)

    def c_bdonesR(self):
        return self.conR[:, 128:256].bitcast(F32R)

    def proj_fm(self, w, wB, c0, m, hT, hB, t0, evac):
        cfg, S = self.cfg, self.S
        KT, MB = cfg.KT, cfg.TH
        bank = self.next_bank()
        for kt in range(KT):
            S.op("pe", lambda e, kt=kt, bank=bank: e.matmul(self.psum[bank][:m, :MB], lhsT=w[:, kt, c0:c0 + m], rhs=hT[:, kt, t0:t0 + MB],
                                                            start=(kt == 0), stop=(kt == KT - 1)),
                 reads=[wB, hB[kt]], writes=[self.psb[bank]])
        evac(self.psum[bank][:m, :MB], self.psb[bank])

    def proj_shift(self, w, wB, c0, m, hT, hB, t0, cidx, mucol, dst, dstB):
        cfg, S = self.cfg, self.S
        MB = cfg.TH
        P, PB = self.Pt[self._pi % 2]
        Dt, DB = self.Dt[self._pi % 2]
        self._pi += 1
        S.op("pool", lambda e: e.tensor_copy(out=P[:m, 0:1], in_=self.carry[:m, cidx:cidx + 1]), reads=[self.carryB], writes=[PB])
        self.proj_fm(w, wB, c0, m, hT, hB, t0,
                     lambda ps, psB: S.op("act", lambda e: e.activation(out=P[:m, 1:MB + 1], in_=ps, func=AF.Copy), reads=[psB], writes=[PB]))
        S.op("pool", lambda e: e.tensor_copy(out=self.carry[:m, cidx:cidx + 1], in_=P[:m, MB:MB + 1]), reads=[PB], writes=[self.carryB])
        S.op("dve", lambda e: e.tensor_tensor(out=Dt[:m, :], in0=P[:m, 0:MB], in1=P[:m, 1:MB + 1], op=ALU.subtract), reads=[PB], writes=[DB])
        S.op("dve", lambda e: e.scalar_tensor_tensor(out=dst, in0=Dt[:m, :], scalar=self.mv[:m, mucol:mucol + 1], in1=P[:m, 1:MB + 1],
                                                     op0=ALU.mult, op1=ALU.add), reads=[DB, PB, self.mvB], writes=[dstB])

    def mixer_block(self, s, hb, hT, hB, o_all):
        cfg, S, A = self.cfg, self.S, self.A
        KT, MB = cfg.KT, cfg.TH
        t0 = hb * MB
        NCK = MB // 64
        NTT = MB // 128
        A.push()
        self._pi = 0
        self.Pt = [self.tl("Pt", [128, MB + 1], F32) for _ in range(2)]
        self.Dt = [self.tl("Dt", [128, MB], F32) for _ in range(2)]
        mv = self.mv
        cmask = self.con[:, 960:960 + MB]
        adT, adB = self.tl("adT", [16, MB], F32)
        twd, twdB = self.tl("twd", [96, MB], F32)
        adS, adSB = self.tl("adS", [96, MB], F32)
        sgd, sgdB = self.tl("sgd", [128, 2, MB], F32)
        A.push()
        wl, wlB = self.tl("wl", [128, KT, 464], BF16)
        S.op("pool", lambda e: e.dma_start(out=wl[:].rearrange("p k c -> p (k c)"), in_=self.w_lora_d), writes=[wlB], dma=True)
        tmp, tmpB = self.tl("ltmp", [128, MB], F32)
        self.proj_fm(wl, wlB, 0, 16, hT, hB, t0,
                     lambda ps, psB: S.op("act", lambda e: e.activation(out=adT[:].bitcast(F32R), in_=ps, func=AF.Copy), reads=[psB], writes=[adB]))
        self.proj_shift(wl, wlB, 16, 96, hT, hB, t0, 0, MV["mu_wd"], tmp[:96, :], tmpB)
        S.op("act", lambda e: e.activation(out=twd[:].bitcast(F32R), in_=tmp[:96, :], func=AF.Tanh), reads=[tmpB], writes=[twdB])
        self.proj_shift(wl, wlB, 112, 96, hT, hB, t0, 1, MV["mu_ad"], tmp[:96, :], tmpB)
        S.op("act", lambda e: e.activation(out=adS[:].bitcast(F32R), in_=tmp[:96, :], func=AF.Copy), reads=[tmpB], writes=[adSB])
        for g2 in range(2):
            self.proj_shift(wl, wlB, 208 + 128 * g2, 128, hT, hB, t0, 2 + g2, MV["mu_gd"] + g2, tmp[:, :], tmpB)
            S.op("act", lambda e, g2=g2: e.activation(out=sgd[:, g2, :].bitcast(F32R), in_=tmp[:, :], func=AF.Sigmoid), reads=[tmpB], writes=[sgdB])
        A.pop()
        self.barrier()
        self.stop("lora")
        A.push()
        self._cache = {}
        for h in range(4):
            self.gla_head(s, h, hT, hB, t0, adT, adB, cmask, o_all)
            self.stop("gla")
        A.pop()
        self.barrier()
        A.push()
        self._cache = {}
        for pp in range(8):
            self.rw_pair(s, pp, hT, hB, t0, twd, twdB, adS, adSB, sgd, sgdB, cmask, o_all)
            self.stop("rw")
        A.pop()
        self.barrier()
        A.pop()
        self.barrier()

    def gla_head(self, s, h, hT, hB, t0, adT, adB, cmask, o_all):
        cfg, S, A = self.cfg, self.S, self.A
        KT, MB = cfg.KT, cfg.TH
        NCK, NTT = MB // 64, MB // 128
        mv = self.mv
        w, wB = self.ct("wg", [128, KT, 768], BF16)
        S.op("pool", lambda e: e.dma_start(out=w[:].rearrange("p k c -> p (k c)"), in_=self.w_gla_d[h]), writes=[wB], dma=True)
        qT, qB = self.ct("qT", [128, MB], F32)
        kT, kB = self.ct("kT", [128, MB], F32)
        rS, rSB = self.ct("rS", [128, 2, MB], F32)
        vtm, vtmB = self.ct("vtm", [128, NTT, 256], F32)
        L, LB = self.ct("L", [128, MB], F32)
        CL, CLB = self.ct("CL", [128, MB], F32)
        KD, KDB = self.ct("KD", [128, MB], F32)
        kdtm, kdtmB = self.ct("kdtm", [128, NTT, 128], F32)
        ET, ETB = self.ct("ET", [128, NCK], F32)
        self.proj_fm(w, wB, 0, 128, hT, hB, t0,
                     lambda ps, psB: S.op("act", lambda e: e.activation(out=qT[:].bitcast(F32R), in_=ps, func=AF.Copy, scale=128.0 ** -0.5),
                                          reads=[psB], writes=[qB]))
        self.proj_fm(w, wB, 128, 128, hT, hB, t0,
                     lambda ps, psB: S.op("act", lambda e: e.activation(out=kT[:], in_=ps, func=AF.Copy), reads=[psB], writes=[kB]))
        for dvt in range(2):
            self.proj_fm(w, wB, 512 + 128 * dvt, 128, hT, hB, t0,
                         lambda ps, psB, dvt=dvt: S.op("act", lambda e: e.activation(out=rS[:, dvt, :], in_=ps, func=AF.Silu), reads=[psB], writes=[rSB]))
        for tt in range(NTT):
            bank, c0, bufs = self.small(2)
            for kt in range(KT):
                S.op("pe", lambda e, kt=kt, tt=tt, bank=bank, c0=c0: e.matmul(bank[:, c0:c0 + 256], lhsT=hT[:, kt, t0 + tt * 128:t0 + (tt + 1) * 128],
                                                                               rhs=w[:, kt, 256:512], start=(kt == 0), stop=(kt == KT - 1)),
                     reads=[wB, hB[kt]], writes=bufs)
            S.op("act", lambda e, tt=tt, bank=bank, c0=c0: e.activation(out=vtm[:, tt, :].bitcast(F32R), in_=bank[:, c0:c0 + 256], func=AF.Copy),
                 reads=bufs, writes=[vtmB])
        bank = self.next_bank()
        S.op("pe", lambda e, bank=bank: e.matmul(self.psum[bank][:, :MB], lhsT=self.gwa2[0:16, h * 128:(h + 1) * 128].bitcast(F32R),
                                                 rhs=adT[0:16, :].bitcast(F32R), start=True, stop=True),
             reads=[self.gwa2B, adB], writes=[self.psb[bank]])
        S.op("act", lambda e, bank=bank: e.activation(out=L[:], in_=self.psum[bank][:, :MB], func=AF.Exp, scale=-1.0, bias=self.mvd[:, h:h + 1]),
             reads=[self.psb[bank], self.mvdB], writes=[LB])
        S.op("act", lambda e: e.activation(out=L[:], in_=L[:], func=AF.Ln, bias=1.0, scale=1.0), reads=[LB], writes=[LB])
        S.op("dve", lambda e: e.tensor_tensor_scan(out=CL[:], data0=cmask, data1=L[:], initial=0.0, op0=ALU.mult, op1=ALU.add),
             reads=[LB, self.conB], writes=[CLB])
        CLv = CL[:].rearrange("p (c j) -> p c j", j=64)
        S.op("act", lambda e: e.activation(out=ET[:], in_=CLv[:, :, 63], func=AF.Exp, scale=-1.0 / 16.0), reads=[CLB], writes=[ETB])
        S.op("dve", lambda e: e.tensor_tensor(out=L[:].rearrange("p (c j) -> p c j", j=64), in0=CLv[:, :, 63:64].broadcast_to([128, NCK, 64]),
                                              in1=CLv, op=ALU.subtract), reads=[CLB], writes=[LB])
        S.op("act", lambda e: e.activation(out=L[:], in_=L[:], func=AF.Exp, scale=-1.0 / 16.0), reads=[LB], writes=[LB])
        S.op("dve", lambda e: e.tensor_tensor(out=KD[:], in0=kT[:], in1=L[:], op=ALU.mult), reads=[kB, LB], writes=[KDB])
        for tt in range(NTT):
            bank, c0, bufs = self.small(1)
            S.op("pe", lambda e, tt=tt, bank=bank, c0=c0: e.transpose(bank[:, c0:c0 + 128], KD[:, tt * 128:(tt + 1) * 128], self.c_ident()),
                 reads=[KDB, self.conB], writes=bufs)
            S.op("act", lambda e, tt=tt, bank=bank, c0=c0: e.activation(out=kdtm[:, tt, :].bitcast(F32R), in_=bank[:, c0:c0 + 128], func=AF.Copy),
                 reads=bufs, writes=[kdtmB])
        oT, oB = self.ct("oT", [128, 2, MB], F32)
        sq, sqB = self.ct("sq", [128, 2, MB], F32)
        Sg, SgB = self.Sg[h]
        obank = [0, 1]
        for c in range(NCK):
            tt, rows = c // 2, (c % 2) * 64
            cur = self.gcur[h]
            new = 1 - cur
            self.gcur[h] = new
            bank, c0, bufs = self.small(2)
            S.op("pe", lambda e, tt=tt, rows=rows, bank=bank, c0=c0: e.matmul(bank[:, c0:c0 + 256], lhsT=kdtm[rows:rows + 64, tt, :].bitcast(F32R),
                                                                              rhs=vtm[rows:rows + 64, tt, :].bitcast(F32R), start=True, stop=True),
                 reads=[kdtmB, vtmB], writes=bufs)
            S.op("dve", lambda e, c=c, cur=cur, new=new, bank=bank, c0=c0: e.scalar_tensor_tensor(
                out=Sg[:, new, :].bitcast(F32R), in0=Sg[:, cur, :], scalar=ET[:, c:c + 1], in1=bank[:, c0:c0 + 256], op0=ALU.mult, op1=ALU.add),
                reads=bufs + [SgB, ETB], writes=[SgB])
            for dvt in range(2):
                S.op("pe", lambda e, c=c, new=new, dvt=dvt: e.matmul(self.psum[obank[dvt]][:, c * 64:(c + 1) * 64],
                                                                     lhsT=Sg[:, new, dvt * 128:(dvt + 1) * 128].bitcast(F32R),
                                                                     rhs=qT[:, c * 64:(c + 1) * 64].bitcast(F32R), start=True, stop=True),
                     reads=[SgB, qB], writes=[self.psb[obank[dvt]]])
        for dvt in range(2):
            S.op("act", lambda e, dvt=dvt: e.activation(out=oT[:, dvt, :], in_=self.psum[obank[dvt]][:, :MB], func=AF.Copy),
                 reads=[self.psb[obank[dvt]]], writes=[oB])
            S.op("act", lambda e, dvt=dvt: e.activation(out=sq[:, dvt, :].bitcast(F32R), in_=self.psum[obank[dvt]][:, :MB], func=AF.Square),
                 reads=[self.psb[obank[dvt]]], writes=[sqB])
        bank = self.next_bank()
        for dvt in range(2):
            S.op("pe", lambda e, dvt=dvt, bank=bank: e.matmul(self.psum[bank][:, :MB], lhsT=self.onesR[:], rhs=sq[:, dvt, :].bitcast(F32R),
                                                              start=(dvt == 0), stop=(dvt == 1)),
                 reads=[sqB, self.onesB], writes=[self.psb[bank]])
        S.op("act", lambda e, bank=bank: e.activation(out=L[:], in_=self.psum[bank][:, :MB], func=AF.Sqrt, scale=1.0 / 256.0, bias=self.eps_ap[:]),
             reads=[self.psb[bank], self.epsB], writes=[LB])
        S.op("dve", lambda e: e.reciprocal(out=L[:], in_=L[:]), reads=[LB], writes=[LB])
        ob, obB = self.ct("ob", [128, 2, MB], BF16)
        for dvt in range(2):
            S.op("dve", lambda e, dvt=dvt: e.scalar_tensor_tensor(out=oT[:, dvt, :], in0=oT[:, dvt, :], scalar=mv[:, MV["gla_gn"] + dvt:MV["gla_gn"] + dvt + 1],
                                                                  in1=L[:], op0=ALU.mult, op1=ALU.mult), reads=[oB, LB, self.mvB], writes=[oB])
            S.op("dve", lambda e, dvt=dvt: e.tensor_tensor(out=ob[:, dvt, :], in0=oT[:, dvt, :], in1=rS[:, dvt, :], op=ALU.mult),
                 reads=[oB, rSB], writes=[obB])
            ct = h * 2 + dvt
            S.op("sp", lambda e, dvt=dvt, ct=ct: e.dma_start(out=o_all[s, ct, :, t0:t0 + MB], in_=ob[:, dvt, :]), reads=[obB], dma=True)

    def rw_pair(self, s, pp, hT, hB, t0, twd, twdB, adS, adSB, sgd, sgdB, cmask, o_all):
        cfg, S, A = self.cfg, self.S, self.A
        KT, MB = cfg.KT, cfg.TH
        NCK, NTT = MB // 64, MB // 128
        mv = self.mv
        C0 = float(np.exp(-0.5))

        def col(name):
            return mv[:, MV[name] + pp:MV[name] + pp + 1]

        def fm(name, dt=F32):
            return self.ct(name, [128, MB], dt)

        w, wB = self.ct("wr", [128, KT, 384], BF16)
        S.op("pool", lambda e: e.dma_start(out=w[:].rearrange("p k c -> p (k c)"), in_=self.w_rw_d[pp]), writes=[wB], dma=True)
        rT, rB = fm("rT")
        kT, kB = fm("kT")
        vT, vB = fm("vT")
        self.proj_shift(w, wB, 0, 128, hT, hB, t0, 4 + pp * 3 + 0, MV["mu_rkv"] + pp * 3 + 0, rT[:], rB)
        self.proj_shift(w, wB, 128, 128, hT, hB, t0, 4 + pp * 3 + 1, MV["mu_rkv"] + pp * 3 + 1, kT[:], kB)
        self.proj_shift(w, wB, 256, 128, hT, hB, t0, 4 + pp * 3 + 2, MV["mu_rkv"] + pp * 3 + 2, vT[:], vB)
        SG, SGB = fm("SG")
        aT, aB = fm("aT")
        gT, gB = fm("gT")
        cs = slice(pp * 128, (pp + 1) * 128)
        bank = self.next_bank()
        S.op("pe", lambda e, bank=bank: e.matmul(self.psum[bank][:, :MB], lhsT=self.ww2[0:96, cs].bitcast(F32R), rhs=twd[0:96, :].bitcast(F32R),
                                                 start=True, stop=True), reads=[self.ww2B, twdB], writes=[self.psb[bank]])
        S.op("act", lambda e, bank=bank: e.activation(out=SG[:], in_=self.psum[bank][:, :MB], func=AF.Sigmoid, bias=col("w0"), scale=1.0),
             reads=[self.psb[bank], self.mvB], writes=[SGB])
        bank = self.next_bank()
        S.op("pe", lambda e, bank=bank: e.matmul(self.psum[bank][:, :MB], lhsT=self.wa2[0:96, cs].bitcast(F32R), rhs=adS[0:96, :].bitcast(F32R),
                                                 start=True, stop=True), reads=[self.wa2B, adSB], writes=[self.psb[bank]])
        S.op("act", lambda e, bank=bank: e.activation(out=aT[:], in_=self.psum[bank][:, :MB], func=AF.Sigmoid, bias=col("a0"), scale=1.0),
             reads=[self.psb[bank], self.mvB], writes=[aB])
        bank = self.next_bank()
        for g2 in range(2):
            S.op("pe", lambda e, bank=bank, g2=g2: e.matmul(self.psum[bank][:, :MB], lhsT=self.wg2[:, g2, cs].bitcast(F32R), rhs=sgd[:, g2, :].bitcast(F32R),
                                                            start=(g2 == 0), stop=(g2 == 1)), reads=[self.wg2B, sgdB], writes=[self.psb[bank]])
        S.op("act", lambda e, bank=bank: e.activation(out=gT[:], in_=self.psum[bank][:, :MB], func=AF.Copy), reads=[self.psb[bank]], writes=[gB])
        kk, kkB = fm("kk")
        t1, t1B = fm("t1")
        t1r, t1rB = fm("t1r")
        t2, t2B = fm("t2")
        S.op("dve", lambda e: e.tensor_scalar(out=kk[:], in0=kT[:], scalar1=col("k_k"), scalar2=None, op0=ALU.mult), reads=[kB, self.mvB], writes=[kkB])
        S.op("act", lambda e: e.activation(out=t1r[:].bitcast(F32R), in_=kk[:], func=AF.Square), reads=[kkB], writes=[t1rB])
        bank = self.next_bank()
        S.op("pe", lambda e, bank=bank: e.matmul(self.psum[bank][:, :MB], lhsT=self.c_bdonesR(), rhs=t1r[:].bitcast(F32R), start=True, stop=True),
             reads=[t1rB, self.conRB], writes=[self.psb[bank]])
        S.op("act", lambda e, bank=bank: e.activation(out=t2[:], in_=self.psum[bank][:, :MB], func=AF.Sqrt), reads=[self.psb[bank]], writes=[t2B])
        S.op("dve", lambda e: e.tensor_scalar(out=t2[:], in0=t2[:], scalar1=1e-12, scalar2=None, op0=ALU.max), reads=[t2B], writes=[t2B])
        S.op("dve", lambda e: e.reciprocal(out=t2[:], in_=t2[:]), reads=[t2B], writes=[t2B])
        S.op("dve", lambda e: e.tensor_tensor(out=kk[:], in0=kk[:], in1=t2[:], op=ALU.mult), reads=[kkB, t2B], writes=[kkB])
        kp, kpB = fm("kp")
        bT, bB = fm("bT")
        S.op("dve", lambda e: e.tensor_scalar(out=t2[:], in0=aT[:], scalar1=col("k_a"), scalar2=self.mvd[:, 4 + pp:5 + pp], op0=ALU.mult, op1=ALU.add),
             reads=[aB, self.mvB, self.mvdB], writes=[t2B])
        S.op("dve", lambda e: e.tensor_tensor(out=kp[:], in0=kT[:], in1=t2[:], op=ALU.mult), reads=[kB, t2B], writes=[kpB])
        S.op("dve", lambda e: e.tensor_tensor(out=bT[:], in0=kk[:], in1=aT[:], op=ALU.mult), reads=[kkB, aB], writes=[bB])
        BN, BNB = fm("BN")
        S.op("dve", lambda e: e.tensor_tensor(out=t2[:], in0=rT[:], in1=kp[:], op=ALU.mult), reads=[rB, kpB], writes=[t2B])
        S.op("dve", lambda e: e.tensor_scalar(out=t1r[:].bitcast(F32R), in0=t2[:], scalar1=col("r_k"), scalar2=None, op0=ALU.mult),
             reads=[t2B, self.mvB], writes=[t1rB])
        bank = self.next_bank()
        S.op("pe", lambda e, bank=bank: e.matmul(self.psum[bank][:, :MB], lhsT=self.c_bdonesR(), rhs=t1r[:].bitcast(F32R), start=True, stop=True),
             reads=[t1rB, self.conRB], writes=[self.psb[bank]])
        S.op("dve", lambda e, bank=bank: e.tensor_tensor(out=BN[:], in0=self.psum[bank][:, :MB], in1=vT[:], op=ALU.mult),
             reads=[self.psb[bank], vB], writes=[BNB])
        CL, CLB = fm("CL")
        S.op("dve", lambda e: e.tensor_tensor_scan(out=CL[:], data0=cmask, data1=SG[:], initial=0.0, op0=ALU.mult, op1=ALU.add),
             reads=[SGB, self.conB], writes=[CLB])
        CLv = CL[:].rearrange("p (c j) -> p c j", j=64)
        WC, WCB = self.ct("WC", [128, NCK], F32)
        S.op("act", lambda e: e.activation(out=WC[:], in_=CLv[:, :, 63], func=AF.Exp, scale=-C0), reads=[CLB], writes=[WCB])
        KR, KRB = self.ct("KR", [128, 2, MB], F32)
        kd, kdB = fm("kd")
        bd, bdB = fm("bd")
        kdp, kdpB = fm("kdp")
        nbdp, nbdpB = fm("nbdp")
        S.op("act", lambda e: e.activation(out=t1[:], in_=CL[:], func=AF.Exp, scale=-C0), reads=[CLB], writes=[t1B])
        S.op("dve", lambda e: e.tensor_tensor(out=KR[:, 1, :].bitcast(F32R), in0=rT[:], in1=t1[:], op=ALU.mult), reads=[rB, t1B], writes=[KRB])
        S.op("act", lambda e: e.activation(out=t2[:], in_=CL[:], func=AF.Exp, scale=C0), reads=[CLB], writes=[t2B])
        S.op("dve", lambda e: e.tensor_tensor(out=kd[:].bitcast(F32R), in0=kp[:], in1=t2[:], op=ALU.mult), reads=[kpB, t2B], writes=[kdB])
        S.op("dve", lambda e: e.tensor_tensor(out=bd[:].bitcast(F32R), in0=bT[:], in1=t2[:], op=ALU.mult), reads=[bB, t2B], writes=[bdB])
        S.op("dve", lambda e: e.tensor_tensor(out=t1[:], in0=CL[:], in1=SG[:], op=ALU.subtract), reads=[CLB, SGB], writes=[t1B])
        S.op("act", lambda e: e.activation(out=t1[:], in_=t1[:], func=AF.Exp, scale=-C0), reads=[t1B], writes=[t1B])
        S.op("dve", lambda e: e.tensor_tensor(out=KR[:, 0, :].bitcast(F32R), in0=kk[:], in1=t1[:], op=ALU.mult), reads=[kkB, t1B], writes=[KRB])
        S.op("dve", lambda e: e.tensor_tensor(out=t2[:].rearrange("p (c j) -> p c j", j=64), in0=CLv[:, :, 63:64].broadcast_to([128, NCK, 64]),
                                              in1=CLv, op=ALU.subtract), reads=[CLB], writes=[t2B])
        S.op("act", lambda e: e.activation(out=t2[:], in_=t2[:], func=AF.Exp, scale=-C0), reads=[t2B], writes=[t2B])
        S.op("dve", lambda e: e.tensor_tensor(out=kdp[:], in0=kp[:], in1=t2[:], op=ALU.mult), reads=[kpB, t2B], writes=[kdpB])
        S.op("dve", lambda e: e.scalar_tensor_tensor(out=nbdp[:], in0=bT[:], scalar=-1.0, in1=t2[:], op0=ALU.mult, op1=ALU.mult),
             reads=[bB, t2B], writes=[nbdpB])
        YT, YTB = fm("YT")
        DWall, DWallB = self.ct("DWall", [128, NCK, 64], F32)
        S.op("dve", lambda e: e.tensor_tensor(out=DWall[:].bitcast(F32R), in0=self.con[:, 896:960][:, None, :].broadcast_to([128, NCK, 64]),
                                              in1=WC[:, :, None].broadcast_to([128, NCK, 64]), op=ALU.mult),
             reads=[self.conB, WCB], writes=[DWallB])
        DGh = [self.ct("DGh%d" % i_, [64, NCK, 64], F32) for i_ in range(2)]
        for hh in range(2):
            pb = 64 * hh
            bank = self.next_bank()
            S.op("pe", lambda e, pb=pb, bank=bank: e.matmul(self.psum[bank][0:64, 0:NCK * 64], lhsT=self.conR[pb:pb + 64, pb:pb + 64].bitcast(F32R),
                                                            rhs=DWall[pb:pb + 64, :, :].bitcast(F32R), start=True, stop=True),
                 reads=[self.conRB, DWallB], writes=[self.psb[bank]])
            S.op("act", lambda e, hh=hh, bank=bank: e.activation(out=DGh[hh][0][:].rearrange("p c k -> p (c k)"), in_=self.psum[bank][0:64, 0:NCK * 64], func=AF.Copy),
                 reads=[self.psb[bank]], writes=[DGh[hh][1]])
        negMS = self.con[:, 256:384]
        M3m = self.con[:, 384:640]
        M24m = self.con[:, 640:896]
        identpair = self.con[:, 896:960]
        ysa = (self.psum[0], 0, [self.psb[0]])
        ysb = (self.psum[1], 0, [self.psb[1]])

        def rr(gens):
            gens = list(gens)
            while gens:
                for g_ in list(gens):
                    try:
                        next(g_)
                    except StopIteration:
                        gens.remove(g_)

        def tiles_of(tt):
            tmq, tmqB = self.ct("tmq%d" % (tt % 2), [128, 5, 128], F32)
            YL, YLB = self.ct("YL%d" % (tt % 2), [128, 128], F32)
            return tmq, tmqB, YL, YLB

        def hset(tt, hh):
            k = hh + 2 * (tt % 2)
            d = {}
            for nm, shp in (("XZ", [128, 2, 256]), ("ZN", [128, 256]), ("KKRK", [128, 256]), ("Pm", [128, 2, 128]),
                            ("R", [128, 128]), ("UK", [128, 128]), ("QT", [64, 128]), ("PT", [64, 2, 64]), ("Gsb", [64, 2, 64])):
                d[nm] = self.ct("%s%d" % (nm, k), shp, F32)
            return d

        def transposes(tt):
            ts = slice(tt * 128, (tt + 1) * 128)
            tmq, tmqB, YL, YLB = tiles_of(tt)
            for qi, (src, srcB, cc) in enumerate(((KR, KRB, 0), (kdp, kdpB, None), (nbdp, nbdpB, None), (vT, vB, None), (KR, KRB, 1))):
                bank, c0, bufs = self.small(1)
                sap = src[:, cc, ts] if cc is not None else src[:, ts]
                S.op("pe", lambda e, sap=sap, bank=bank, c0=c0: e.transpose(bank[:, c0:c0 + 128], sap, self.c_ident()),
                     reads=[srcB, self.conB], writes=bufs)
                S.op("act", lambda e, qi=qi, bank=bank, c0=c0: e.activation(out=tmq[:, qi, :].bitcast(F32R), in_=bank[:, c0:c0 + 128], func=AF.Copy),
                     reads=bufs, writes=[tmqB])

        def pre(tt, hh):
            ts = slice(tt * 128, (tt + 1) * 128)
            tmq, tmqB, YL, YLB = tiles_of(tt)
            H = hset(tt, hh)
            (XZ, XZB), (ZN, ZNB), (KKRK, KKRKB), (Pm, PmB) = H["XZ"], H["ZN"], H["KKRK"], H["Pm"]
            (R, RB), (UK, UKB), (QT, QTB), (PT, PTB), (Gsb, GsbB) = H["R"], H["UK"], H["QT"], H["PT"], H["Gsb"]
            pb = 64 * hh
            kkw_h = KR[pb:pb + 64, 0, ts].bitcast(F32R)
            bd_h = bd[pb:pb + 64, ts].bitcast(F32R)
            kd_h = kd[pb:pb + 64, ts].bitcast(F32R)
            kr_h = KR[pb:pb + 64, :, ts].bitcast(F32R)
            b1 = self.small(1)
            S.op("pe", lambda e: e.matmul(b1[0][:, b1[1]:b1[1] + 128], lhsT=kkw_h, rhs=bd_h, start=True, stop=True), reads=[KRB, bdB], writes=b1[2])
            b2 = self.small(2)
            S.op("pe", lambda e: e.matmul(b2[0][:, b2[1]:b2[1] + 256], lhsT=bd_h, rhs=kr_h, start=True, stop=True), reads=[KRB, bdB], writes=b2[2])
            b3 = self.small(2)
            S.op("pe", lambda e: e.matmul(b3[0][:, b3[1]:b3[1] + 256], lhsT=kd_h, rhs=kr_h, start=True, stop=True), reads=[KRB, kdB], writes=b3[2])
            S.op("dve", lambda e: e.tensor_tensor(out=XZ[:, 0, 0:128].bitcast(F32R), in0=b1[0][:, b1[1]:b1[1] + 128], in1=negMS, op=ALU.mult),
                 reads=b1[2] + [self.conB], writes=[XZB])
            S.op("dve", lambda e: e.tensor_tensor(out=ZN[:].bitcast(F32R), in0=b2[0][:, b2[1]:b2[1] + 256], in1=M24m, op=ALU.mult),
                 reads=b2[2] + [self.conB], writes=[ZNB])
            S.op("act", lambda e: e.activation(out=XZ[:, 0, 128:256].bitcast(F32R), in_=ZN[:, 0:128], func=AF.Copy), reads=[ZNB], writes=[XZB])
            S.op("dve", lambda e: e.tensor_tensor(out=Pm[:, 0, :].bitcast(F32R), in0=ZN[:, 0:128], in1=self.c_ident(), op=ALU.add),
                 reads=[ZNB, self.conB], writes=[PmB])
            S.op("dve", lambda e: e.tensor_tensor(out=KKRK[:].bitcast(F32R), in0=b3[0][:, b3[1]:b3[1] + 256], in1=M3m, op=ALU.mult),
                 reads=b3[2] + [self.conB], writes=[KKRKB])
            yield
            pc = 0
            for lv in range(1, 7):
                a, b_ = (lv - 1) % 2, lv % 2
                bxz = bp = None
                Xa = XZ[:, a, 0:128].bitcast(F32R)
                Za = XZ[:, a, 128:256].bitcast(F32R)
                if lv <= 5:
                    bxz = self.small(2)
                    S.op("pe", lambda e, bxz=bxz, Xa=Xa, Za=Za: e.matmul(bxz[0][:, bxz[1]:bxz[1] + 128], lhsT=Za, rhs=Xa, start=True, stop=True),
                         reads=[XZB], writes=bxz[2])
                    if lv <= 4:
                        S.op("pe", lambda e, bxz=bxz, Xa=Xa, Za=Za: e.matmul(bxz[0][:, bxz[1] + 128:bxz[1] + 256], lhsT=Xa, rhs=Za, start=True, stop=True),
                             reads=[XZB], writes=bxz[2])
                if lv >= 2:
                    bp = self.small(1)
                    S.op("pe", lambda e, pc=pc, bp=bp, Xa=Xa: e.matmul(bp[0][:, bp[1]:bp[1] + 128], lhsT=Xa, rhs=Pm[:, pc, :].bitcast(F32R),
                                                                       start=True, stop=True), reads=[XZB, PmB], writes=bp[2])
                if bxz is not None:
                    n_ = 256 if lv <= 4 else 128
                    S.op("act", lambda e, b_=b_, bxz=bxz, n_=n_: e.activation(out=XZ[:, b_, 0:n_].bitcast(F32R), in_=bxz[0][:, bxz[1]:bxz[1] + n_], func=AF.Copy),
                         reads=bxz[2], writes=[XZB])
                if bp is not None:
                    S.op("dve", lambda e, pc=pc, bp=bp: e.tensor_tensor(out=Pm[:, 1 - pc, :].bitcast(F32R), in0=bp[0][:, bp[1]:bp[1] + 128],
                                                                        in1=Pm[:, pc, :], op=ALU.add), reads=bp[2] + [PmB], writes=[PmB])
                    pc = 1 - pc
                yield
            S.op("act", lambda e: e.activation(out=R[:, 0:64].bitcast(F32R), in_=tmq[:, 0, pb:pb + 64], func=AF.Copy), reads=[tmqB], writes=[RB])
            bw = self.small(1)
            S.op("pe", lambda e: e.matmul(bw[0][:, bw[1]:bw[1] + 64], lhsT=KKRK[:, 0:128].bitcast(F32R), rhs=tmq[:, 3, pb:pb + 64].bitcast(F32R),
                                          start=True, stop=True), reads=[KKRKB, tmqB], writes=bw[2])
            S.op("act", lambda e: e.activation(out=R[:, 64:128].bitcast(F32R), in_=bw[0][:, bw[1]:bw[1] + 64], func=AF.Copy), reads=bw[2], writes=[RB])
            yield
            bu = self.small(1)
            S.op("pe", lambda e: e.matmul(bu[0][:, bu[1]:bu[1] + 128], lhsT=Pm[:, pc, :].bitcast(F32R), rhs=R[:].bitcast(F32R), start=True, stop=True),
                 reads=[PmB, RB], writes=bu[2])
            S.op("act", lambda e: e.activation(out=UK[:].bitcast(F32R), in_=bu[0][:, bu[1]:bu[1] + 128], func=AF.Copy), reads=bu[2], writes=[UKB])
            yield
            by = self.small(1)
            if hh == 0:
                S.op("pe", lambda e: e.matmul(by[0][0:64, by[1]:by[1] + 128], lhsT=tmq[:, 3, 0:64].bitcast(F32R), rhs=KKRK[:, 128:256].bitcast(F32R),
                                              start=True, stop=False), reads=[tmqB, KKRKB], writes=by[2])
                S.op("pe", lambda e: e.matmul(by[0][0:64, by[1]:by[1] + 128], lhsT=UK[:, 64:128].bitcast(F32R), rhs=ZN[:, 128:256].bitcast(F32R),
                                              start=False, stop=True), reads=[UKB, ZNB], writes=by[2])
            else:
                S.op("pe", lambda e: e.matmul(by[0][:, by[1]:by[1] + 128], lhsT=tmq[:, 3, :].bitcast(F32R), rhs=KKRK[:, 128:256].bitcast(F32R),
                                              start=True, stop=False), reads=[tmqB, KKRKB], writes=by[2])
                S.op("pe", lambda e: e.matmul(by[0][:, by[1]:by[1] + 128], lhsT=UK[:].bitcast(F32R), rhs=ZN[:, 128:256].bitcast(F32R),
                                              start=False, stop=True), reads=[UKB, ZNB], writes=by[2])
            bq = self.small(1)
            S.op("pe", lambda e: e.matmul(bq[0][0:64, bq[1]:bq[1] + 128], lhsT=tmq[:, 4, pb:pb + 64].bitcast(F32R), rhs=self.c_identR(), start=True, stop=False),
                 reads=[tmqB, self.conRB], writes=bq[2])
            S.op("pe", lambda e: e.matmul(bq[0][0:64, bq[1]:bq[1] + 128], lhsT=UK[:, 0:64].bitcast(F32R), rhs=ZN[:, 128:256].bitcast(F32R), start=False, stop=True),
                 reads=[UKB, ZNB], writes=bq[2])
            if hh == 0:
                S.op("act", lambda e: e.activation(out=YL[0:64, :], in_=by[0][0:64, by[1]:by[1] + 128], func=AF.Copy), reads=by[2], writes=[YLB])
            else:
                S.op("act", lambda e: e.activation(out=YL[64:128, :], in_=by[0][64:128, by[1]:by[1] + 128], func=AF.Copy), reads=by[2], writes=[YLB])
            S.op("act", lambda e: e.activation(out=QT[:].bitcast(F32R), in_=bq[0][0:64, bq[1]:bq[1] + 128], func=AF.Copy), reads=bq[2], writes=[QTB])
            for c2 in range(2):
                ck = tt * 2 + c2
                rows = 64 * c2
                bpt = self.small(1)
                S.op("pe", lambda e, rows=rows, bpt=bpt: e.matmul(bpt[0][0:64, bpt[1]:bpt[1] + 64], lhsT=UK[rows:rows + 64, 0:64].bitcast(F32R),
                                                                  rhs=tmq[rows:rows + 64, 2, pb:pb + 64].bitcast(F32R), start=True, stop=True),
                     reads=[UKB, tmqB], writes=bpt[2])
                bg = self.small(1)
                S.op("pe", lambda e, rows=rows, bg=bg: e.matmul(bg[0][0:64, bg[1]:bg[1] + 64], lhsT=tmq[rows:rows + 64, 1, pb:pb + 64].bitcast(F32R),
                                                                rhs=tmq[rows:rows + 64, 3, pb:pb + 64].bitcast(F32R), start=True, stop=False),
                     reads=[tmqB], writes=bg[2])
                S.op("pe", lambda e, rows=rows, bg=bg: e.matmul(bg[0][0:64, bg[1]:bg[1] + 64], lhsT=tmq[rows:rows + 64, 2, pb:pb + 64].bitcast(F32R),
                                                                rhs=UK[rows:rows + 64, 64:128].bitcast(F32R), start=False, stop=True),
                     reads=[tmqB, UKB], writes=bg[2])
                S.op("dve", lambda e, c2=c2, ck=ck, bpt=bpt: e.tensor_tensor(out=PT[:, c2, :].bitcast(F32R), in0=bpt[0][0:64, bpt[1]:bpt[1] + 64],
                                                                             in1=DGh[hh][0][:, ck, :], op=ALU.add), reads=bpt[2] + [DGh[hh][1]], writes=[PTB])
                S.op("act", lambda e, c2=c2, bg=bg: e.activation(out=Gsb[:, c2, :], in_=bg[0][0:64, bg[1]:bg[1] + 64], func=AF.Copy), reads=bg[2], writes=[GsbB])
            yield

        def chain(tt, hh):
            H = hset(tt, hh)
            (QT, QTB), (PT, PTB), (Gsb, GsbB) = H["QT"], H["PT"], H["Gsb"]
            hd = pp * 2 + hh
            Tt, TtB = self.Tst[hd]
            tc0 = 64 * hh
            ysbank, ysc0, ysbufs = ysa if hh == 0 else ysb
            for c2 in range(2):
                cur = self.tcur[hd]
                new = 1 - cur
                self.tcur[hd] = new
                if hh == 0:
                    S.op("pe", lambda e, cur=cur, c2=c2: e.matmul(ysbank[0:64, ysc0 + 64 * c2:ysc0 + 64 * (c2 + 1)], lhsT=Tt[:, cur, 0:64].bitcast(F32R),
                                                                  rhs=QT[:, 64 * c2:64 * (c2 + 1)].bitcast(F32R), start=True, stop=True),
                         reads=[TtB, QTB], writes=ysbufs)
                else:
                    S.op("pe", lambda e, cur=cur, c2=c2: e.matmul(ysbank[:, ysc0 + 64 * c2:ysc0 + 64 * (c2 + 1)], lhsT=Tt[:, cur, :].bitcast(F32R),
                                                                  rhs=QT[:, 64 * c2:64 * (c2 + 1)].bitcast(F32R), start=True, stop=True),
                         reads=[TtB, QTB], writes=ysbufs)
                bc = self.small(1)
                S.op("pe", lambda e, c2=c2, cur=cur, bc=bc: e.matmul(bc[0][0:64, bc[1]:bc[1] + 64], lhsT=PT[:, c2, :].bitcast(F32R),
                                                                     rhs=Tt[:, cur, tc0:tc0 + 64].bitcast(F32R), start=True, stop=True),
                     reads=[PTB, TtB], writes=bc[2])
                S.op("dve", lambda e, c2=c2, new=new, bc=bc: e.tensor_tensor(out=Tt[:, new, tc0:tc0 + 64].bitcast(F32R), in0=bc[0][0:64, bc[1]:bc[1] + 64],
                                                                             in1=Gsb[:, c2, :], op=ALU.add), reads=bc[2] + [GsbB], writes=[TtB])
                yield

        for tt0 in range(0, NTT, 2):
            tts = [t_ for t_ in (tt0, tt0 + 1) if t_ < NTT]
            for tt in tts:
                transposes(tt)
            rr([pre(tt, hh) for tt in tts for hh in range(2)])
            for tt in tts:
                ts = slice(tt * 128, (tt + 1) * 128)
                tmq, tmqB, YL, YLB = tiles_of(tt)
                rr([chain(tt, 0), chain(tt, 1)])
                S.op("dve", lambda e, ts=ts, YL=YL: e.tensor_tensor(out=YT[0:64, ts].bitcast(F32R), in0=ysa[0][0:64, ysa[1]:ysa[1] + 128], in1=YL[0:64, :], op=ALU.add),
                     reads=ysa[2] + [YLB], writes=[YTB])
                S.op("dve", lambda e, ts=ts, YL=YL: e.tensor_tensor(out=YT[64:128, ts].bitcast(F32R), in0=ysb[0][64:128, ysb[1]:ysb[1] + 128], in1=YL[64:128, :], op=ALU.add),
                     reads=ysb[2] + [YLB], writes=[YTB])
        bank = self.next_bank()
        S.op("pe", lambda e, bank=bank: e.matmul(self.psum[bank][:, :MB], lhsT=self.c_bdonesR(), rhs=YT[:].bitcast(F32R), start=True, stop=True),
             reads=[YTB, self.conRB], writes=[self.psb[bank]])
        S.op("act", lambda e: e.activation(out=t1r[:].bitcast(F32R), in_=YT[:], func=AF.Square), reads=[YTB], writes=[t1rB])
        bank2 = self.next_bank()
        S.op("pe", lambda e, bank2=bank2: e.matmul(self.psum[bank2][:, :MB], lhsT=self.c_bdonesR(), rhs=t1r[:].bitcast(F32R), start=True, stop=True),
             reads=[t1rB, self.conRB], writes=[self.psb[bank2]])
        mean, meanB = fm("kk")
        S.op("act", lambda e, bank=bank: e.activation(out=mean[:], in_=self.psum[bank][:, :MB], func=AF.Copy, scale=1.0 / 64.0),
             reads=[self.psb[bank]], writes=[meanB])
        S.op("dve", lambda e: e.tensor_tensor(out=t2[:], in0=mean[:], in1=mean[:], op=ALU.mult), reads=[meanB], writes=[t2B])
        S.op("dve", lambda e, bank2=bank2: e.scalar_tensor_tensor(out=t2[:], in0=self.psum[bank2][:, :MB], scalar=1.0 / 64.0, in1=t2[:],
                                                                  op0=ALU.mult, op1=ALU.subtract), reads=[self.psb[bank2], t2B], writes=[t2B])
        S.op("act", lambda e: e.activation(out=t2[:], in_=t2[:], func=AF.Sqrt, bias=self.gneps[:], scale=1.0), reads=[t2B, self.gnepsB], writes=[t2B])
        S.op("dve", lambda e: e.reciprocal(out=t2[:], in_=t2[:]), reads=[t2B], writes=[t2B])
        S.op("dve", lambda e: e.tensor_tensor(out=t1[:], in0=YT[:], in1=mean[:], op=ALU.subtract), reads=[YTB, meanB], writes=[t1B])
        S.op("dve", lambda e: e.tensor_tensor(out=t1[:], in0=t1[:], in1=t2[:], op=ALU.mult), reads=[t1B, t2B], writes=[t1B])
        S.op("dve", lambda e: e.tensor_scalar(out=t1[:], in0=t1[:], scalar1=col("lnx_w"), scalar2=col("lnx_b"), op0=ALU.mult, op1=ALU.add),
             reads=[t1B, self.mvB], writes=[t1B])
        S.op("dve", lambda e: e.tensor_tensor(out=t1[:], in0=t1[:], in1=BN[:], op=ALU.add), reads=[t1B, BNB], writes=[t1B])
        ob, obB = self.ct("obr", [128, MB], BF16)
        S.op("dve", lambda e: e.tensor_tensor(out=ob[:], in0=t1[:], in1=gT[:], op=ALU.mult), reads=[t1B, gB], writes=[obB])
        S.op("sp", lambda e: e.dma_start(out=o_all[s, 8 + pp, :, t0:t0 + MB], in_=ob[:]), reads=[obB], dma=True)

    def build(self, mode="full"):
        try:
            return self._build(mode)
        except _Stop:
            self.S.emit()
            return self.nc

    def _build(self, mode="full"):
        cfg, S, A, nc = self.cfg, self.S, self.A, self.nc
        KT, T, TH, NH, FT, FQ, NQ = cfg.KT, cfg.T, cfg.TH, cfg.NH, cfg.FT, cfg.FQ, cfg.NQ
        DG = cfg.DG
        x_in = self.din("x_t", [4, 128, KT, T])
        vecs = self.din("vecs", [128, 4 * KT])
        wgu1 = self.din("wgu1", [FT, 128, 2 * KT * 128])
        wd1 = self.din("wd1", [NQ, KT // DG, 128, DG, FQ * 128])
        wgu2 = self.din("wgu2", [FT, 128, 2 * KT * 128])
        wd2 = self.din("wd2", [NQ, KT // DG, 128, DG, FQ * 128])
        self.w_lora_d = self.din("w_lora", [128, KT * 464])
        self.w_gla_d = self.din("w_gla", [4, 128, KT * 768])
        self.w_rw_d = self.din("w_rw", [8, 128, KT * 384])
        wc_d = self.din("w_c", [KT, 128, KT * 256 + 16 * 128])
        wo_d = self.din("w_o", [KT, 128, KT * 128])
        gb_d = self.din("gate_bt", [128, 2 * KT])
        out = self.dout("out_t", [128, KT, T])
        x1_all = nc.dram_tensor("x1_all", [4, 128, KT, T], F32, kind="Internal").ap()
        o_all = nc.dram_tensor("o_all", [4, 16, 128, T], BF16, kind="Internal").ap()
        if mode == "mixer":
            o_dbg = self.dout("o_dbg", [4, 16, 128, T])

        vT = A.alloc("vecs", [128, 4 * KT], F32)
        self.vecB = Buf("vecs")
        self.onesR = A.alloc("onesR", [128, 128], F32R)
        self.onesB = Buf("ones")
        self.eps_ap = A.alloc("eps", [128, 1], F32)
        self.epsB = Buf("eps")
        ones32 = A.alloc("ones32", [128, 128], F32)
        o32B = Buf("ones32")
        S.op("dve", lambda e: e.memset(ones32[:], 1.0), writes=[o32B])
        S.op("dve", lambda e: e.tensor_copy(out=self.onesR[:], in_=ones32[:]), reads=[o32B], writes=[self.onesB])
        S.op("dve", lambda e: e.memset(self.eps_ap[:], NORM_EPS), writes=[self.epsB])
        S.op("sp", lambda e: e.dma_start(out=vT[:], in_=vecs), writes=[self.vecB], dma=True)
        self.mixer_setup()
        self.barrier()
        self.stop("setup")

        for s in range(4):
            A.push()
            h2T = A.alloc("h2T", [128, KT, T], BF16)
            h2B = [Buf("h2_%d" % k) for k in range(KT)]
            A.push()
            xT = A.alloc("xT", [128, KT, T], F32)
            xB = [[Buf("x%d_%d" % (k, h)) for h in range(NH)] for k in range(KT)]
            for kt in range(KT):
                S.op("sp", lambda e, kt=kt, s=s: e.dma_start(out=xT[:, kt, :], in_=x_in[s, :, kt, :]), writes=[xB[kt][h] for h in range(NH)], dma=True)
            if mode == "dbg1":
                self.rmsnorm_fm(xT, xB, vT[:, KT:2 * KT], h2T, h2B)
                S.emit()
                return nc
            if mode != "mixer_only":
                self.ffn(xT, xB, vT[:, 0:KT], wgu1, wd1, hT=h2T, hB=h2B)
            for kt in range(KT):
                S.op("sp", lambda e, kt=kt, s=s: e.dma_start(out=x1_all[s, :, kt, :], in_=xT[:, kt, :]), reads=[xB[kt][h] for h in range(NH)], dma=True)
            self.rmsnorm_fm(xT, xB, vT[:, KT:2 * KT], h2T, h2B)
            A.pop()
            self.barrier()
            self.stop("ffn1")
            for hb in range(NH):
                self.mixer_block(s, hb, h2T, h2B, o_all)
            A.pop()
            self.barrier()

        A.pop()
        self.barrier()
        if mode == "mixer":
            A.push()
            for s in range(4):
                for ct in range(16):
                    tb, tbB = self.tl("dbg", [128, T], BF16)
                    tf, tfB = self.tl("dbgf", [128, T], F32)
                    S.op("sp", lambda e, s=s, ct=ct, tb=tb: e.dma_start(out=tb[:], in_=o_all[s, ct]), writes=[tbB], dma=True)
                    S.op("dve", lambda e, tb=tb, tf=tf: e.tensor_copy(out=tf[:], in_=tb[:]), reads=[tbB], writes=[tfB])
                    S.op("sp", lambda e, s=s, ct=ct, tf=tf: e.dma_start(out=o_dbg[s, ct], in_=tf[:]), reads=[tfB], dma=True)
                    self.A.cur -= 0
            A.pop()

        sel0 = MV["sel"]
        xT = A.alloc("xT", [128, KT, T], F32)
        xB = [[Buf("x%d_%d" % (k, h)) for h in range(NH)] for k in range(KT)]
        A.push()
        oT = A.alloc("oTc", [128, 16, T], BF16)
        oB = [Buf("oc%d" % k) for k in range(16)]
        A.push()
        tx = [self.tl("tx", [128, T], F32) for _ in range(2)]
        tb = [self.tl("tb", [128, T], BF16) for _ in range(2)]
        i = 0
        for kt in range(KT):
            for s in range(4):
                t_, tB_ = tx[i % 2]
                i += 1
                S.op("sp", lambda e, kt=kt, s=s, t_=t_: e.dma_start(out=t_[:], in_=x1_all[s, :, kt, :]), writes=[tB_], dma=True)
                if s == 0:
                    S.op("dve", lambda e, kt=kt, t_=t_: e.tensor_scalar(out=xT[:, kt, :], in0=t_[:], scalar1=self.mv[:, sel0:sel0 + 1], scalar2=None, op0=ALU.mult),
                         reads=[tB_, self.mvB], writes=[xB[kt][h] for h in range(NH)])
                else:
                    S.op("dve", lambda e, kt=kt, s=s, t_=t_: e.scalar_tensor_tensor(out=xT[:, kt, :], in0=t_[:], scalar=self.mv[:, sel0 + s:sel0 + s + 1],
                                                                                    in1=xT[:, kt, :], op0=ALU.mult, op1=ALU.add),
                         reads=[tB_, self.mvB] + [xB[kt][h] for h in range(NH)], writes=[xB[kt][h] for h in range(NH)])
        i = 0
        for ct in range(16):
            for s in range(4):
                t_, tB_ = tb[i % 2]
                i += 1
                S.op("sp", lambda e, ct=ct, s=s, t_=t_: e.dma_start(out=t_[:], in_=o_all[s, ct]), writes=[tB_], dma=True)
                if s == 0:
                    S.op("pool", lambda e, ct=ct, t_=t_: e.tensor_scalar(out=oT[:, ct, :], in0=t_[:], scalar1=self.mv[:, sel0:sel0 + 1], scalar2=None, op0=ALU.mult),
                         reads=[tB_, self.mvB], writes=[oB[ct]])
                else:
                    S.op("dve", lambda e, ct=ct, s=s, t_=t_: e.scalar_tensor_tensor(out=oT[:, ct, :], in0=t_[:], scalar=self.mv[:, sel0 + s:sel0 + s + 1],
                                                                                    in1=oT[:, ct, :], op0=ALU.mult, op1=ALU.add),
                         reads=[tB_, self.mvB, oB[ct]], writes=[oB[ct]])
        A.pop()
        self.barrier()
        hT = A.alloc("hTc", [128, KT, T], BF16)
        hB = [Buf("hc%d" % k) for k in range(KT)]
        self.rmsnorm_fm(xT, xB, vT[:, KT:2 * KT], hT, hB)
        mT = A.alloc("mT", [128, KT, T], BF16)
        mB = [Buf("m%d" % k) for k in range(KT)]
        gbt, gbtB = self.tl("gbt", [128, 2 * KT], F32)
        S.op("sp", lambda e: e.dma_start(out=gbt[:], in_=gb_d), writes=[gbtB], dma=True)
        WCW = KT * 256 + 16 * 128
        wcs = [self.tl("wc", [128, WCW], BF16) for _ in range(2)]
        sgt = [self.tl("sgt", [128, TH], F32) for _ in range(2)]
        m1t = [self.tl("m1t", [128, TH], F32) for _ in range(2)]
        for dt in range(KT):
            wc, wcB = wcs[dt % 2]
            S.op("pool", lambda e, dt=dt, wc=wc: e.dma_start(out=wc[:], in_=wc_d[dt]), writes=[wcB], dma=True)
            for h in range(NH):
                hs = slice(h * TH, (h + 1) * TH)
                pb = 4 * ((dt * NH + h) % 2)
                for br in range(2):
                    for kt in range(KT):
                        S.op("pe", lambda e, kt=kt, br=br, wc=wc, pb=pb, hs=hs: e.matmul(self.psum[pb + br][:, :TH], lhsT=wc[:, kt * 256 + br * 128:kt * 256 + (br + 1) * 128],
                                                                                        rhs=hT[:, kt, hs], start=(kt == 0), stop=(kt == KT - 1)),
                             reads=[wcB, hB[kt]], writes=[self.psb[pb + br]])
                    for k8 in range(8):
                        ct = br * 8 + k8
                        S.op("pe", lambda e, ct=ct, k8=k8, br=br, wc=wc, pb=pb, hs=hs: e.matmul(self.psum[pb + 2 + br][:, :TH],
                                                                                               lhsT=wc[:, KT * 256 + ct * 128:KT * 256 + (ct + 1) * 128],
                                                                                               rhs=oT[:, ct, hs], start=(k8 == 0), stop=(k8 == 7)),
                             reads=[wcB, oB[ct]], writes=[self.psb[pb + 2 + br]])
                sg0, sg0B = sgt[0]
                sg1, sg1B = sgt[1]
                m1, m1B = m1t[0]
                m2, m2B = m1t[1]
                S.op("act", lambda e, dt=dt, pb=pb: e.activation(out=sg0[:], in_=self.psum[pb][:, :TH], func=AF.Sigmoid, bias=gbt[:, dt:dt + 1], scale=1.0),
                     reads=[self.psb[pb], gbtB], writes=[sg0B])
                S.op("act", lambda e, dt=dt, pb=pb: e.activation(out=sg1[:], in_=self.psum[pb + 1][:, :TH], func=AF.Sigmoid, bias=gbt[:, KT + dt:KT + dt + 1], scale=1.0),
                     reads=[self.psb[pb + 1], gbtB], writes=[sg1B])
                S.op("dve", lambda e, pb=pb: e.tensor_tensor(out=m1[:], in0=sg0[:], in1=self.psum[pb + 2][:, :TH], op=ALU.mult),
                     reads=[sg0B, self.psb[pb + 2]], writes=[m1B])
                S.op("dve", lambda e, pb=pb: e.tensor_tensor(out=m2[:], in0=sg1[:], in1=self.psum[pb + 3][:, :TH], op=ALU.mult),
                     reads=[sg1B, self.psb[pb + 3]], writes=[m2B])
                S.op("dve", lambda e, dt=dt, hs=hs: e.tensor_tensor(out=mT[:, dt, hs], in0=m1[:], in1=m2[:], op=ALU.add),
                     reads=[m1B, m2B], writes=[mB[dt]])
        wos = [self.tl("wo", [128, KT * 128], BF16) for _ in range(2)]
        for dt in range(KT):
            wo, woB = wos[dt % 2]
            S.op("pool", lambda e, dt=dt, wo=wo: e.dma_start(out=wo[:], in_=wo_d[dt]), writes=[woB], dma=True)
            for h in range(NH):
                hs = slice(h * TH, (h + 1) * TH)
                bank = (dt * NH + h) % 8
                for kt in range(KT):
                    S.op("pe", lambda e, kt=kt, wo=wo, bank=bank, hs=hs: e.matmul(self.psum[bank][:, :TH], lhsT=wo[:, kt * 128:(kt + 1) * 128], rhs=mT[:, kt, hs],
                                                                                   start=(kt == 0), stop=(kt == KT - 1)),
                         reads=[woB, mB[kt]], writes=[self.psb[bank]])
                S.op("dve", lambda e, dt=dt, bank=bank, hs=hs, h=h: e.tensor_tensor(out=xT[:, dt, hs], in0=self.psum[bank][:, :TH], in1=xT[:, dt, hs], op=ALU.add),
                     reads=[self.psb[bank], xB[dt][h]], writes=[xB[dt][h]])
        A.pop()
        self.barrier()

        self.ffn(xT, xB, vT[:, 2 * KT:3 * KT], wgu2, wd2)

        A.push()
        oTf = A.alloc("oTf", [128, KT, T], F32)
        oBf = [Buf("of%d" % k) for k in range(KT)]
        self.rmsnorm_fm(xT, xB, vT[:, 3 * KT:4 * KT], oTf, oBf)
        for kt in range(KT):
            S.op("sp", lambda e, kt=kt: e.dma_start(out=out[:, kt, :], in_=oTf[:, kt, :]), reads=[oBf[kt]], dma=True)
        S.wait_all("sp", [t for t in S.rings["sp"].last if t is not None])
        S.q["sp"].ops.append(([], None, None))
        A.pop()
        S.emit()
        return nc


MV = {"gla_ba": 0, "gla_gn": 4, "mu_wd": 6, "mu_ad": 7, "mu_gd": 8, "mu_rkv": 10, "w0": 34, "a0": 42, "k_k": 50, "k_a": 58,
      "r_k": 66, "lnx_w": 74, "lnx_b": 82, "sel": 90}
MV_N = 94

GLA_QK, GLA_V, GLA_LORA = 512, 1024, 16
GLA_IN = 2 * GLA_QK + 2 * GLA_V + GLA_LORA
RWKV_W = 1024
RWKV_IN = 3 * RWKV_W + 96 + 96 + 256


def fm_vec(v, KT):
    return np.ascontiguousarray(v.reshape(KT, 128).T)


def prep_ffn(wg, wu, wd, cfg):
    KT, FT, FQ, NQ, DG = cfg.KT, cfg.FT, cfg.FQ, cfg.NQ, cfg.DG
    g = wg.reshape(KT, 128, FT, 128).transpose(2, 1, 0, 3)
    u = wu.reshape(KT, 128, FT, 128).transpose(2, 1, 0, 3)
    wgu = np.ascontiguousarray(np.stack([g, u], axis=2)).reshape(FT, 128, 2 * KT * 128)
    w = wd.reshape(NQ, FQ, 128, KT // DG, DG, 128).transpose(0, 3, 2, 4, 1, 5)
    wdt = np.ascontiguousarray(w).reshape(NQ, KT // DG, 128, DG, FQ * 128)
    return wgu, wdt


def pkc(wcols, KT):
    C = wcols.shape[1]
    return np.ascontiguousarray(wcols.reshape(KT, 128, C).transpose(1, 0, 2)).reshape(128, KT * C)


def col128(v, n):
    out = np.zeros((128, n), np.float32)
    L = v.shape[0]
    if n == 1:
        out[:L, 0] = v
    else:
        out[:, :] = v.reshape(n, 128).T
    return out


def make_consts(MB):
    ident = np.eye(128, dtype=np.float32)
    bd = np.zeros((128, 128), np.float32)
    bd[:64, :64] = 1
    bd[64:, 64:] = 1
    p = np.arange(128)[:, None]
    f = np.arange(128)[None, :]
    same = (p // 64) == (f // 64)
    negMS = np.where(same & (p > f), -1.0, 0.0).astype(np.float32)
    MST = np.where(same & (f > p), 1.0, 0.0).astype(np.float32)
    MIT = np.where(same & (f >= p), 1.0, 0.0).astype(np.float32)
    identpair = np.concatenate([np.eye(64, dtype=np.float32)] * 2, axis=0)
    cmask = np.ones((128, MB), np.float32)
    cmask[:, 0::64] = 0
    return np.ascontiguousarray(np.concatenate([ident, bd, negMS, MST, MIT, -MST, -MIT, identpair, cmask], axis=1))


def make_in_maps(inputs, cfg, n_cores=8):
    D, T, KT = cfg.D, cfg.T, cfg.KT
    f32 = lambda k: np.asarray(inputs[k], dtype=np.float32)
    x = f32("x")
    vecs = np.concatenate([fm_vec(f32(k).reshape(-1), KT) for k in ("ffn1_norm", "mix_norm", "ffn2_norm", "final_norm")], axis=1)
    wgu1, wd1 = prep_ffn(f32("ffn1_wg")[0], f32("ffn1_wu")[0], f32("ffn1_wd")[0], cfg)
    wgu2, wd2 = prep_ffn(f32("ffn2_wg")[0], f32("ffn2_wu")[0], f32("ffn2_wd")[0], cfg)
    w_in = f32("w_in")[0]
    g0, r0, gt0 = 0, GLA_IN, GLA_IN + RWKV_IN
    lora_cols = np.concatenate([w_in[:, 3072:3088], w_in[:, r0 + 3072:r0 + 3168], w_in[:, r0 + 3168:r0 + 3264], w_in[:, r0 + 3264:r0 + 3520]], axis=1)
    w_lora = pkc(lora_cols, KT)
    w_gla = np.stack([pkc(np.concatenate([w_in[:, h * 128:(h + 1) * 128], w_in[:, 512 + h * 128:512 + (h + 1) * 128],
                                          w_in[:, 1024 + h * 256:1024 + (h + 1) * 256], w_in[:, 2048 + h * 256:2048 + (h + 1) * 256]], axis=1), KT)
                      for h in range(4)])
    w_rw = np.stack([pkc(np.concatenate([w_in[:, r0 + m * 1024 + pp * 128:r0 + m * 1024 + (pp + 1) * 128] for m in range(3)], axis=1), KT)
                     for pp in range(8)])
    wb = f32("w_branch")[0]
    wout = f32("w_out")[0]
    w_c = []
    w_o = []
    for dt in range(KT):
        gcols = np.concatenate([w_in[:, gt0 + dt * 128:gt0 + (dt + 1) * 128], w_in[:, gt0 + D + dt * 128:gt0 + D + (dt + 1) * 128]], axis=1)
        bcols = wb[:, dt * 128:(dt + 1) * 128]
        w_c.append(np.concatenate([pkc(gcols, KT), pkc(bcols, 16)], axis=1))
        w_o.append(pkc(wout[:, dt * 128:(dt + 1) * 128], KT))
    w_c = np.ascontiguousarray(np.stack(w_c))
    w_o = np.ascontiguousarray(np.stack(w_o))
    gb = f32("gate_b")[0]
    gate_bt = np.ascontiguousarray(np.concatenate([fm_vec(gb[:D], KT), fm_vec(gb[D:], KT)], axis=1))
    mu = f32("rwkv_mu")[0]
    mvb = np.zeros((128, MV_N), np.float32)
    mvb[:, MV["gla_ba"]:MV["gla_ba"] + 4] = col128(f32("gla_b_a")[0], 4)
    mvb[:, MV["gla_gn"]:MV["gla_gn"] + 2] = col128(f32("gla_gn_w")[0], 2)
    mvb[:, MV["mu_wd"]:MV["mu_wd"] + 1] = col128(mu[3072:3168], 1)
    mvb[:, MV["mu_ad"]:MV["mu_ad"] + 1] = col128(mu[3168:3264], 1)
    mvb[:, MV["mu_gd"]:MV["mu_gd"] + 2] = col128(mu[3264:3520], 2)
    for pp in range(8):
        for m in range(3):
            mvb[:, MV["mu_rkv"] + pp * 3 + m] = mu[m * 1024 + pp * 128:m * 1024 + (pp + 1) * 128]
    for nm, key in (("w0", "rwkv_w0"), ("a0", "rwkv_a0"), ("k_k", "rwkv_k_k"), ("k_a", "rwkv_k_a"), ("r_k", "rwkv_r_k"),
                    ("lnx_w", "rwkv_lnx_w"), ("lnx_b", "rwkv_lnx_b")):
        mvb[:, MV[nm]:MV[nm] + 8] = col128(f32(key)[0].reshape(-1), 8)
    consts = make_consts(cfg.TH)
    gla_wa2 = np.ascontiguousarray(f32("gla_w_a2")[0])
    rw_ww2 = np.ascontiguousarray(f32("rwkv_w_w2")[0])
    rw_wa2 = np.ascontiguousarray(f32("rwkv_w_a2")[0])
    rw_wg2 = np.ascontiguousarray(f32("rwkv_w_g2")[0].reshape(2, 128, 1024).transpose(1, 0, 2))
    maps = []
    xb_cache = {}
    for c in range(n_cores):
        b, j = c // 4, c % 4
        if b not in xb_cache:
            xs = x[b].reshape(4, T, KT, 128)
            xb_cache[b] = np.ascontiguousarray(xs.transpose(0, 3, 2, 1))
        mvc = mvb.copy()
        mvc[:, MV["sel"] + j] = 1.0
        maps.append({"x_t": xb_cache[b], "vecs": np.ascontiguousarray(vecs), "wgu1": wgu1, "wd1": wd1, "wgu2": wgu2, "wd2": wd2,
                     "w_lora": w_lora, "w_gla": w_gla, "w_rw": w_rw, "w_c": w_c, "w_o": w_o, "gate_bt": gate_bt, "mvecs": mvc,
                     "consts": consts, "gla_wa2": gla_wa2, "rw_ww2": rw_ww2, "rw_wa2": rw_wa2, "rw_wg2": rw_wg2})
    return maps


def assemble(results, cfg, n_cores=8):
    D, T, KT = cfg.D, cfg.T, cfg.KT
    out = np.zeros((n_cores // 4, 4 * T, D), np.float32)
    for c in range(n_cores):
        b, j = c // 4, c % 4
        o = np.asarray(results[c]["out_t"]).reshape(128, KT, T)
        out[b, j * T:(j + 1) * T, :] = o.transpose(2, 1, 0).reshape(T, D)
    return out


_CACHE = {}


def kernel(**inputs):
    cfg = Cfg()
    if "nc" not in _CACHE:
        _CACHE["nc"] = Prog(cfg).build()
    nc = _CACHE["nc"]
    maps = make_in_maps(inputs, cfg)
    res = run_bass_kernel_spmd(nc, maps, core_ids=list(range(8)))
    return assemble(res.results, cfg)
```
